# Optimizing a Trainium2 kernel written in Bass

```python
import jax, jax.numpy as jnp
from jax import lax
import numpy as np

D_MODEL = 1024
BATCH = 16
SEQ = 2048
DEPTH = 4

CHUNK = 64
Q_BLOCK = 128
ROPE_THETA = 10000.0
LN_EPS = 1e-5
RMS_EPS = 1e-6

A_HEADS = 8
A_HEAD_DIM = 64
IDX_HEADS = 4
IDX_DIM = 64
TOPK_MAX = 256

B_HEADS = 8
B_NOPE = 64
B_ROPE = 32
B_V = 64
B_Q_RANK = 384
B_KV_RANK = 256

C_HEADS = 16
C_HEAD_DIM = D_MODEL // C_HEADS

D_FF = 2816
CONV_WIDTH = 3

EVEN_IN_SIZES = (A_HEADS * A_HEAD_DIM, A_HEAD_DIM, A_HEAD_DIM, IDX_HEADS * IDX_DIM, IDX_DIM, IDX_HEADS, B_Q_RANK, B_KV_RANK, B_ROPE)
EVEN_IN_DIM = sum(EVEN_IN_SIZES)
EVEN_OUT_DIM = A_HEADS * A_HEAD_DIM + B_HEADS * B_V
N_EVEN = (DEPTH + 1) // 2
N_ODD = DEPTH // 2
DEEPNORM_ALPHA = (2 * DEPTH) ** 0.25
DEEPNORM_BETA = (8 * DEPTH) ** -0.25

kernel_name = 'hybrid_dsa_mla_stickbreak_convffn_deepnorm'


def split_sizes(x, sizes):
    idx = np.cumsum(sizes)[:-1].tolist()
    return jnp.split(x, idx, axis=-1)


def layer_norm(x, g, b):
    xf = x.astype(jnp.float32)
    mu = xf.mean(-1, keepdims=True)
    var = jnp.square(xf - mu).mean(-1, keepdims=True)
    y = (xf - mu) * lax.rsqrt(var + LN_EPS)
    return (y * g.astype(jnp.float32) + b.astype(jnp.float32)).astype(x.dtype)


def rms_norm(x, g):
    xf = x.astype(jnp.float32)
    y = xf * lax.rsqrt(jnp.mean(xf * xf, -1, keepdims=True) + RMS_EPS)
    return (y * g.astype(jnp.float32)).astype(x.dtype)


def rope_tables(positions, dim):
    inv_freq = ROPE_THETA ** (-jnp.arange(0, dim, 2, dtype=jnp.float32) / dim)
    ang = positions.astype(jnp.float32)[..., None] * inv_freq
    return jnp.cos(ang), jnp.sin(ang)


def apply_rope(x, cos, sin):
    if x.ndim == 4:
        cos, sin = cos[:, :, None, :], sin[:, :, None, :]
    c = cos.astype(x.dtype)
    s = sin.astype(x.dtype)
    x1, x2 = jnp.split(x, 2, axis=-1)
    return jnp.concatenate([x1 * c - x2 * s, x2 * c + x1 * s], axis=-1)


def to_blocks(x):
    b, s = x.shape[:2]
    return jnp.moveaxis(x.reshape(b, s // Q_BLOCK, Q_BLOCK, *x.shape[2:]), 1, 0)


def from_blocks(y):
    y = jnp.moveaxis(y, 0, 1)
    return y.reshape(y.shape[0], -1, *y.shape[3:])


def dsa_attention(q, k, v, iq, ik, iw, frame):
    seq = k.shape[1]
    topk = min(TOPK_MAX, seq // 4)
    key_chunk = frame // CHUNK

    def block(args):
        qb, iqb, iwb, tb = args
        q_chunk = tb // CHUNK
        admissible = key_chunk[None, :] <= q_chunk[:, None]
        rel = jax.nn.relu(jnp.einsum('bthd,bsd->bths', iqb, ik).astype(jnp.float32) * IDX_DIM ** -0.5)
        score = jnp.einsum('bths,bth->bts', rel, iwb.astype(jnp.float32))
        score = jnp.where(admissible[None], score, -jnp.inf)
        _, idx = lax.top_k(score, topk)
        valid = (idx // CHUNK) <= q_chunk[None, :, None]
        k_sel = jax.vmap(lambda kb, ib: kb[ib])(k, idx)
        v_sel = jax.vmap(lambda vb, ib: vb[ib])(v, idx)
        logits = jnp.einsum('bthd,btkd->bthk', qb, k_sel).astype(jnp.float32) * A_HEAD_DIM ** -0.5
        logits = jnp.where(valid[:, :, None, :], logits, -jnp.inf)
        p = jax.nn.softmax(logits, axis=-1).astype(v.dtype)
        return jnp.einsum('bthk,btkd->bthd', p, v_sel)

    out = lax.map(block, (to_blocks(q), to_blocks(iq), to_blocks(iw), frame.reshape(-1, Q_BLOCK)))
    return from_blocks(out)


def mla_attention(q, k_nope, k_rope, v, frame):
    key_chunk = frame // CHUNK
    scale = (B_NOPE + B_ROPE) ** -0.5

    def block(args):
        qb, tb = args
        logits = (jnp.einsum('bthd,bshd->bhts', qb[..., :B_NOPE], k_nope)
                  + jnp.einsum('bthd,bsd->bhts', qb[..., B_NOPE:], k_rope)).astype(jnp.float32) * scale
        mask = key_chunk[None, :] <= (tb // CHUNK)[:, None]
        logits = jnp.where(mask, logits, -jnp.inf)
        p = jax.nn.softmax(logits, axis=-1).astype(v.dtype)
        return jnp.einsum('bhts,bshd->bthd', p, v)

    return from_blocks(lax.map(block, (to_blocks(q), frame.reshape(-1, Q_BLOCK))))


def stick_breaking_attention(q, k, v, frame):
    scale = C_HEAD_DIM ** -0.5

    def block(args):
        qb, tb = args
        z = jnp.einsum('bthd,bshd->bhts', qb, k).astype(jnp.float32) * scale
        before = frame[None, :] < tb[:, None]
        log_1m_beta = jnp.where(before, -jax.nn.softplus(z), 0.0)
        tail = lax.cumsum(log_1m_beta, axis=3, reverse=True) - log_1m_beta
        a = jnp.where(before, jnp.exp(jax.nn.log_sigmoid(z) + tail), 0.0).astype(v.dtype)
        return jnp.einsum('bhts,bshd->bthd', a, v)

    return from_blocks(lax.map(block, (to_blocks(q), frame.reshape(-1, Q_BLOCK))))


def even_mixer(h, positions, frame, w_in, idx_k_g, idx_k_b, q_norm_g, kv_norm_g, w_uq, w_ukv, w_out):
    b, s, _ = h.shape
    qa, ka, va, iq, ik, iw, cq, ckv, kr = split_sizes(h @ w_in, EVEN_IN_SIZES)
    cos_a, sin_a = rope_tables(positions, A_HEAD_DIM)
    cos_i, sin_i = rope_tables(positions, IDX_DIM)
    qa = apply_rope(qa.reshape(b, s, A_HEADS, A_HEAD_DIM), cos_a, sin_a)
    ka = apply_rope(ka, cos_a, sin_a)
    iq = apply_rope(iq.reshape(b, s, IDX_HEADS, IDX_DIM), cos_i, sin_i)
    ik = apply_rope(layer_norm(ik, idx_k_g, idx_k_b), cos_i, sin_i)
    iw = iw * IDX_HEADS ** -0.5
    out_a = dsa_attention(qa, ka, va, iq, ik, iw, frame)
    cos_b, sin_b = rope_tables(positions, B_ROPE)
    qb = (rms_norm(cq, q_norm_g) @ w_uq).reshape(b, s, B_HEADS, B_NOPE + B_ROPE)
    qb = jnp.concatenate([qb[..., :B_NOPE], apply_rope(qb[..., B_NOPE:], cos_b, sin_b)], axis=-1)
    kv = (rms_norm(ckv, kv_norm_g) @ w_ukv).reshape(b, s, B_HEADS, B_NOPE + B_V)
    k_rope = apply_rope(kr, cos_b, sin_b)
    out_b = mla_attention(qb, kv[..., :B_NOPE], k_rope, kv[..., B_NOPE:], frame)
    y = jnp.concatenate([out_a.reshape(b, s, -1), out_b.reshape(b, s, -1)], axis=-1)
    return y @ w_out


def odd_mixer(h, frame, w_qkv, w_out):
    b, s, _ = h.shape
    q, k, v = [t.reshape(b, s, C_HEADS, C_HEAD_DIM) for t in jnp.split(h @ w_qkv, 3, axis=-1)]
    return stick_breaking_attention(q, k, v, frame).reshape(b, s, -1) @ w_out


def conv_ffn(h, w_up, conv_w, conv_b, w_down):
    a, g = jnp.split(h @ w_up, 2, axis=-1)
    a = lax.conv_general_dilated(a, conv_w[:, None, :], window_strides=(1,),
                                 padding=[(CONV_WIDTH - 1, 0)],
                                 dimension_numbers=('NWC', 'WIO', 'NWC'),
                                 feature_group_count=D_FF) + conv_b
    return (jax.nn.silu(a) * g) @ w_down


def setup_inputs(seed: int = 0) -> dict:
    key = jax.random.key(seed)
    ks = iter(jax.random.split(key, 23))
    D = D_MODEL

    def nrm(shape, scale):
        return jax.random.normal(next(ks), shape, jnp.float32) * scale

    x = nrm((BATCH, SEQ, D), 1.0)
    c = nrm((BATCH, D), 1.0)
    offsets = jax.random.randint(next(ks), (BATCH, 1), 0, 64, dtype=jnp.int32) * CHUNK
    positions = (offsets + jnp.arange(SEQ, dtype=jnp.int32)[None, :]).astype(jnp.int32)
    return {
        'x': x,
        'c': c,
        'positions': positions,
        'mod_w': nrm((DEPTH, D, 6 * D), 0.1 * D ** -0.5),
        'mod_b': nrm((DEPTH, 6 * D), 0.02),
        'ln_mix_g': 1.0 + nrm((DEPTH, D), 0.02),
        'ln_mix_b': nrm((DEPTH, D), 0.02),
        'ln_ffn_g': 1.0 + nrm((DEPTH, D), 0.02),
        'ln_ffn_b': nrm((DEPTH, D), 0.02),
        'ev_w_in': nrm((N_EVEN, D, EVEN_IN_DIM), D ** -0.5),
        'ev_idx_k_g': 1.0 + nrm((N_EVEN, IDX_DIM), 0.02),
        'ev_idx_k_b': nrm((N_EVEN, IDX_DIM), 0.02),
        'ev_q_norm_g': 1.0 + nrm((N_EVEN, B_Q_RANK), 0.02),
        'ev_kv_norm_g': 1.0 + nrm((N_EVEN, B_KV_RANK), 0.02),
        'ev_w_uq': nrm((N_EVEN, B_Q_RANK, B_HEADS * (B_NOPE + B_ROPE)), B_Q_RANK ** -0.5),
        'ev_w_ukv': nrm((N_EVEN, B_KV_RANK, B_HEADS * (B_NOPE + B_V)), B_KV_RANK ** -0.5),
        'ev_w_out': nrm((N_EVEN, EVEN_OUT_DIM, D), DEEPNORM_BETA * EVEN_OUT_DIM ** -0.5),
        'od_w_qkv': nrm((N_ODD, D, 3 * C_HEADS * C_HEAD_DIM), D ** -0.5),
        'od_w_out': nrm((N_ODD, C_HEADS * C_HEAD_DIM, D), DEEPNORM_BETA * (C_HEADS * C_HEAD_DIM) ** -0.5),
        'ffn_w_up': nrm((DEPTH, D, 2 * D_FF), D ** -0.5),
        'ffn_conv_w': nrm((DEPTH, CONV_WIDTH, D_FF), CONV_WIDTH ** -0.5),
        'ffn_conv_b': nrm((DEPTH, D_FF), 0.02),
        'ffn_w_down': nrm((DEPTH, D_FF, D), DEEPNORM_BETA * D_FF ** -0.5),
    }


def reference(x, c, positions, mod_w, mod_b, ln_mix_g, ln_mix_b, ln_ffn_g, ln_ffn_b,
              ev_w_in, ev_idx_k_g, ev_idx_k_b, ev_q_norm_g, ev_kv_norm_g, ev_w_uq, ev_w_ukv, ev_w_out,
              od_w_qkv, od_w_out, ffn_w_up, ffn_conv_w, ffn_conv_b, ffn_w_down):
    seq = x.shape[1]
    frame = jnp.arange(seq, dtype=jnp.int32)
    c_act = jax.nn.silu(c)
    for l in range(DEPTH):
        mod = (c_act @ mod_w[l] + mod_b[l])[:, None, :]
        sh_m, sc_m, g_m, sh_f, sc_f, g_f = jnp.split(mod, 6, axis=-1)
        h = x * (1 + sc_m) + sh_m
        if l % 2 == 0:
            i = l // 2
            y = even_mixer(h, positions, frame, ev_w_in[i], ev_idx_k_g[i], ev_idx_k_b[i],
                           ev_q_norm_g[i], ev_kv_norm_g[i], ev_w_uq[i], ev_w_ukv[i], ev_w_out[i])
        else:
            i = l // 2
            y = odd_mixer(h, frame, od_w_qkv[i], od_w_out[i])
        x = layer_norm(DEEPNORM_ALPHA * x + (1 + g_m) * y, ln_mix_g[l], ln_mix_b[l])
        h = x * (1 + sc_f) + sh_f
        y = conv_ffn(h, ffn_w_up[l], ffn_conv_w[l], ffn_conv_b[l], ffn_w_down[l])
        x = layer_norm(DEEPNORM_ALPHA * x + (1 + g_f) * y, ln_ffn_g[l], ln_ffn_b[l])
    return x
```

```python
import math
from contextlib import ExitStack

import numpy as np
import concourse.bass as bass
import concourse.mybir as mybir
from concourse.bass_utils import run_bass_kernel_spmd

F32, BF16, I32 = mybir.dt.float32, mybir.dt.bfloat16, mybir.dt.int32
AF = mybir.ActivationFunctionType
ALU = mybir.AluOpType

D = 1024
DFF = 2816
NFC = DFF // 128
ALPHA = 8 ** 0.25
LN_EPS = 1e-5
RMS_EPS = 1e-6
NEG = -1.0e30
NEG2 = -3.0e30
MASKV = -30000.0
TWO_PI = 2.0 * math.pi


class Res:
    __slots__ = ("lw", "rd")

    def __init__(self):
        self.lw = None
        self.rd = {}


class Eng:
    def __init__(self, name, sid):
        self.name = name
        self.sid = sid
        self.cnt = 0
        self.seen = {}
        self.ops = []


class Kern:
    NDS = 8

    def __init__(self, nc, stack):
        self.nc = nc
        self.sems = []
        self.stack = stack
        self.E = {}
        for n in ("pe", "act", "dve", "pool", "sp"):
            self.E[n] = Eng(n, self._mk("s_" + n))
        self.dsem = {q: [self._mk(f"d_{q}{i}") for i in range(self.NDS)] for q in ("sp", "pool", "act")}
        self.dval = {}
        self.dnext = {"sp": 0, "pool": 0, "act": 0}
        self.nops = 0

    def _mk(self, name):
        h = self.stack.enter_context(self.nc.semaphore(name))
        self.sems.append(h)
        return len(self.sems) - 1

    def _deps(self, E, R, W):
        need = {}
        for r in R:
            if r.lw is not None:
                sid, v = r.lw
                if need.get(sid, 0) < v:
                    need[sid] = v
        for w in W:
            if w.lw is not None:
                sid, v = w.lw
                if sid != E.sid and need.get(sid, 0) < v:
                    need[sid] = v
            for sid, v in w.rd.items():
                if sid != E.sid and need.get(sid, 0) < v:
                    need[sid] = v
        for sid, v in need.items():
            if sid == E.sid and E.name == "pe":
                continue
            if E.seen.get(sid, 0) >= v:
                continue
            E.seen[sid] = v
            E.ops.append(("w", sid, v))
            self.nops += 1

    def op(self, en, fn, R=(), W=()):
        E = self.E[en]
        self._deps(E, R, W)
        E.cnt += 1
        E.ops.append(("o", fn))
        self.nops += 1
        for r in R:
            if r.rd.get(E.sid, 0) < E.cnt:
                r.rd[E.sid] = E.cnt
        tok = (E.sid, E.cnt)
        for w in W:
            w.lw = tok
            w.rd = {}

    def dma(self, q, out, in_, R=(), W=(), slow=False):
        E = self.E[q]
        pool = self.dsem[q]
        i = self.dnext[q]
        self.dnext[q] = (i + 1) % len(pool)
        sid = pool[i]
        prev = self.dval.get(sid, 0)
        if prev > 0 and E.seen.get(sid, 0) < prev:
            E.seen[sid] = prev
            E.ops.append(("w", sid, prev))
        self._deps(E, R, W)
        self.dval[sid] = prev + 16
        E.ops.append(("d", out, in_, sid, slow))
        self.nops += 1
        for r in R:
            r.rd[sid] = prev + 16
        tok = (sid, prev + 16)
        for w in W:
            w.lw = tok
            w.rd = {}

    def barrier(self):
        for E in self.E.values():
            for F in self.E.values():
                if F is E or F.cnt == 0:
                    continue
                if E.seen.get(F.sid, 0) < F.cnt:
                    E.seen[F.sid] = F.cnt
                    E.ops.append(("w", F.sid, F.cnt))
                    self.nops += 1
            for sid, v in self.dval.items():
                if E.seen.get(sid, 0) < v:
                    E.seen[sid] = v
                    E.ops.append(("w", sid, v))
                    self.nops += 1

    def flush(self, final=False):
        nc = self.nc
        if final:
            E = self.E["sp"]
            for sid, v in self.dval.items():
                if E.seen.get(sid, 0) < v:
                    E.seen[sid] = v
                    E.ops.append(("w", sid, v))
        sems = self.sems

        def run(E, eng):
            own = sems[E.sid]
            for o in E.ops:
                if o[0] == "w":
                    eng.wait_ge(sems[o[1]], o[2])
                elif o[0] == "o":
                    o[1](eng).then_inc(own, 1)
                else:
                    if o[4]:
                        eng.dma_start(out=o[1], in_=o[2], allow_slow_non_contiguous=True).then_inc(sems[o[3]], 16)
                    else:
                        eng.dma_start(out=o[1], in_=o[2]).then_inc(sems[o[3]], 16)
            E.ops = []

        with nc.Block() as blk:
            @blk.tensor
            def _(e):
                run(self.E["pe"], e)

            @blk.scalar
            def _(e):
                run(self.E["act"], e)

            @blk.vector
            def _(e):
                run(self.E["dve"], e)

            @blk.gpsimd
            def _(e):
                run(self.E["pool"], e)

            @blk.sync
            def _(e):
                run(self.E["sp"], e)


def mm(K, out, lhsT, rhs, start, stop, R, W):
    K.op("pe", lambda e: e.matmul(out, lhsT, rhs, start=start, stop=stop), R, W)


def tr(K, out, in_, ident, R, W):
    K.op("pe", lambda e: e.transpose(out, in_, ident), R, W)


def act(K, out, in_, func, R, W, bias=None, scale=None):
    kw = {}
    if bias is not None:
        kw["bias"] = bias
    if scale is not None:
        kw["scale"] = scale
    K.op("act", lambda e: e.activation(out, in_, func, **kw), R, W)


def ts(K, en, out, in0, s1, s2, op0, op1, R, W):
    if op1 is None:
        K.op(en, lambda e: e.tensor_scalar(out, in0, s1, None, op0), R, W)
    else:
        K.op(en, lambda e: e.tensor_scalar(out, in0, s1, s2, op0, op1), R, W)


def tt(K, en, out, in0, in1, op, R, W):
    K.op(en, lambda e: e.tensor_tensor(out, in0, in1, op), R, W)


def stt(K, out, in0, scalar, in1, op0, op1, R, W):
    K.op("dve", lambda e: e.scalar_tensor_tensor(out, in0, scalar, in1, op0, op1), R, W)


def cpy(K, en, out, in_, R, W):
    if en == "act":
        K.op("act", lambda e: e.activation(out, in_, AF.Copy), R, W)
    else:
        K.op(en, lambda e: e.tensor_copy(out, in_), R, W)


def recip(K, out, in_, R, W):
    K.op("dve", lambda e: e.reciprocal(out, in_), R, W)


def mset(K, en, ap, val, W):
    K.op(en, lambda e: e.memset(ap, val), (), W)


def build(S, NSEQ, kinds, TOPK):
    L = len(kinds)
    NE = max(1, sum(1 for k in kinds if k == "even"))
    NO = max(1, sum(1 for k in kinds if k == "odd"))
    NT = S // 128
    NR = S // 512
    nc = bass.Bass("TRN2", target_bir_lowering=False)

    def din(name, shape, dt=F32):
        return nc.dram_tensor(name, list(shape), dt, kind="ExternalInput").ap()

    x_d = din("x", [NSEQ, S, D])
    cT_d = din("cT", [128, 8, NSEQ])
    pos_d = din("pos", [NSEQ, S], I32)
    modw_d = din("mod_w", [L, D, 6 * D])
    modb_d = din("mod_bT", [L, 128, 48])
    lnp_d = din("lnp", [L, 4, D])
    win_d = din("ev_win", [NE, D, NX_COLS])
    evs_d = din("ev_small", [NE, 128, 16])
    wuq_d = din("ev_wuq", [NE, 384, 768])
    wuqs_d = din("ev_wuq_sw", [NE, 384, 768])
    wukk_d = din("ev_wukv_k", [NE, 256, 512])
    wukv_d = din("ev_wukv_v", [NE, 256, 512])
    ewo_d = din("ev_wout", [NE, D, D])
    oqkv_d = din("od_wqkv", [NO, D, 3 * D])
    owo_d = din("od_wout", [NO, D, D])
    wup_d = din("ffn_wup", [L, D, 2 * DFF])
    fcw_d = din("ffn_cw", [L, 128, NFC, 3])
    fcb_d = din("ffn_cb", [L, 128, NFC])
    wdn_d = din("ffn_wdown", [L, DFF, D])
    cf_d = din("cf32", [128, 3 * 128 + 8])
    cb_d = din("cbf", [128, 4 * 128 + 512])
    out_d = nc.dram_tensor("out", [NSEQ, S, D], F32, kind="ExternalOutput").ap()
    tabs_d = nc.dram_tensor("rope_tabs", [NSEQ, 4, 128, S], BF16).ap()

    with ExitStack() as st:
        K = Kern(nc, st)

        def sb(name, shape, dt):
            return st.enter_context(nc.sbuf_tensor(name, list(shape), dt))

        X = sb("X", [128, NT, D], F32)
        HT = sb("HT", [128, 8, S], BF16)
        CF = sb("CF", [128, 3 * 128 + 8], F32)
        CB = sb("CB", [128, 4 * 128 + 512], BF16)
        MODT = sb("MODT", [128, L, 48, NSEQ], F32)
        MODB = sb("MODB", [128, L, 48], F32)
        CACT = sb("CACT", [128, 8, NSEQ], F32)
        STAT = sb("STAT", [128, 32], F32)
        PH_F = 87 * 256
        PH = sb("PH", [128, PH_F], F32)
        ph_off = [0]
        VT2 = [None]

        def ph_reset(to=0):
            K.barrier()
            ph_off[0] = to
            VT2[0] = None

        def ph(nelem, dt):
            nf = nelem if dt == F32 else (nelem + 1) // 2
            o = ph_off[0]
            ph_off[0] = o + nf
            assert ph_off[0] <= PH_F, ("phase region overflow", ph_off[0], PH_F)
            a = PH[:, o:o + nf]
            return a if dt == F32 else a.bitcast(BF16)

        PS2 = [st.enter_context(nc.psum_tensor(f"ps{i}", [128, 1024], F32)) for i in range(4)]
        PSB = [PS2[k // 2][:, (k % 2) * 512:(k % 2 + 1) * 512] for k in range(8)]
        rP = [Res() for _ in range(8)]

        IDENT = CF[:, 0:128]
        ONESF = CF[:, 128:256]
        BLK64 = CF[:, 256:384]
        FREQA = CF[:, 384:385]
        SGNA = CF[:, 385:386]
        FREQB = CF[:, 386:387]
        SGNB = CF[:, 387:388]
        ONESB = CB[:, 0:128]
        NEGONES = CB[:, 128:256]
        NEGU = CB[:, 256:384]
        TRI = CB[:, 384:512]
        IREP = CB[:, 512:1024]

        rX = [Res() for _ in range(NT)]
        rTABD = [Res() for _ in range(NSEQ)]
        rHT = [Res() for _ in range(NT)]
        rC = Res()
        rMOD = Res()
        rSTAT = Res()

        K.dma("sp", CF[:], cf_d[:, :], (), (rC,))
        K.dma("pool", CB[:], cb_d[:, :], (), (rC,))
        K.dma("sp", CACT[:], cT_d[:, :, :], (), (rMOD,))
        K.dma("sp", MODB[:], modb_d.rearrange("l p c -> p l c"), (), (rMOD,), slow=True)
        act(K, CACT[:], CACT[:], AF.Silu, (rMOD,), (rMOD,))
        ph_reset()
        MW = [ph(2048, F32), ph(2048, F32)]
        rMW = [Res(), Res()]
        NG = 6 * D // 256
        for l in range(L):
            for g in range(NG):
                slot = (l * NG + g) % 2
                mwv = MW[slot].rearrange("p (k n) -> p k n", k=8)
                K.dma("sp" if g % 2 == 0 else "act", mwv,
                      modw_d[l][:, g * 256:(g + 1) * 256].rearrange("(k p) n -> p k n", p=128),
                      (), (rMW[slot],))
                for j in range(2):
                    ch = g * 2 + j
                    for k in range(8):
                        mm(K, PSB[0][:, ch * NSEQ:(ch + 1) * NSEQ], mwv[:, k, j * 128:(j + 1) * 128],
                           CACT[:, k, :], k == 0, k == 7, (rMW[slot], rMOD), (rP[0],))
            tt(K, "dve", MODT[:, l, :, :], PSB[0][:, 0:48 * NSEQ].rearrange("p (c s) -> p c s", s=NSEQ),
               MODB[:, l, :].unsqueeze(2).to_broadcast([128, 48, NSEQ]), ALU.add, (rP[0], rMOD), (rMOD,))
        for l in range(L):
            for lo in (8, 32):
                ts(K, "dve", MODT[:, l, lo:lo + 16, :], MODT[:, l, lo:lo + 16, :], 1.0, None, ALU.add, None,
                   (rMOD,), (rMOD,))
        K.flush()

        TRB = (6, 7)

        def transpose_mod(i, l, sub, s):
            base = 0 if sub == 0 else 24
            for half in range(2):
                b = TRB[half]
                for j in range(4):
                    ch = half * 4 + j
                    tr(K, PSB[b][:, j * 128:(j + 1) * 128], X[:, i, ch * 128:(ch + 1) * 128], IDENT,
                       (rX[i], rC), (rP[b],))
                for j in range(4):
                    ch = half * 4 + j
                    act(K, HT[:, ch, i * 128:(i + 1) * 128], PSB[b][:, j * 128:(j + 1) * 128], AF.Identity,
                        (rP[b], rMOD), (rHT[i],),
                        bias=MODT[:, l, base + ch, s:s + 1], scale=MODT[:, l, base + 8 + ch, s:s + 1])

        BCG = sb("BCG", [128, D], F32)
        BCLG = sb("BCLG", [128, D], F32)
        BCLB = sb("BCLB", [128, D], F32)
        VT = sb("VT", [128, 1, D], F32)
        DGT = sb("DGT", [128, 1, 512], F32)
        rBC = Res()
        rVT = [Res(), Res()]
        rDG = [Res(), Res()]

        def setup_epilogue(l, sub, s):
            gbase = 16 if sub == 0 else 40
            for half in range(2):
                b = TRB[half]
                for j in range(4):
                    ch = half * 4 + j
                    dg = DGT[:, 0, j * 128:(j + 1) * 128]
                    rr = rDG[0]
                    ts(K, "dve", dg, IDENT, MODT[:, l, gbase + ch, s:s + 1], None, ALU.mult, None, (rC, rMOD), (rr,))
                    mm(K, PSB[b][:, j * 128:(j + 1) * 128], ONESF, dg, True, True, (rr, rC), (rP[b],))
                cpy(K, "act", BCG[:, half * 512:(half + 1) * 512], PSB[b][:, :], (rP[b],), (rBC,))
            gi = 0 if sub == 0 else 2
            K.dma("sp", BCLG[:], lnp_d[l, gi:gi + 1, :].to_broadcast([128, D]), (), (rBC,))
            K.dma("sp", BCLB[:], lnp_d[l, gi + 1:gi + 2, :].to_broadcast([128, D]), (), (rBC,))

        def epilogue(i, ybanks, s, nxt, last, alpha=ALPHA):
            v = VT[:, 0, :] if (i % 2 == 0 or VT2[0] is None) else VT2[0]
            rv = rVT[0] if (i % 2 == 0 or VT2[0] is None) else rVT[1]
            for h in range(2):
                tt(K, "dve", v[:, h * 512:(h + 1) * 512], PSB[ybanks[h]][:, :], BCG[:, h * 512:(h + 1) * 512],
                   ALU.mult, (rP[ybanks[h]], rBC), (rv,))
            stt(K, v, X[:, i, :], alpha, v, ALU.mult, ALU.add, (rX[i], rv), (rv,))
            so = (i % 2) * 16
            for h in range(2):
                K.op("dve", lambda e, h=h: e.bn_stats(STAT[:, so + h * 6: so + h * 6 + 6], v[:, h * 512:(h + 1) * 512]),
                     (rv,), (rSTAT,))
            K.op("dve", lambda e: e.bn_aggr(STAT[:, so + 12: so + 14], STAT[:, so: so + 12]), (rSTAT,), (rSTAT,))
            act(K, STAT[:, so + 14: so + 15], STAT[:, so + 13: so + 14], AF.Sqrt, (rSTAT,), (rSTAT,), bias=EPS_LN[:, 0:1])
            recip(K, STAT[:, so + 14: so + 15], STAT[:, so + 14: so + 15], (rSTAT,), (rSTAT,))
            stt(K, STAT[:, so + 15: so + 16], STAT[:, so + 12: so + 13], -1.0, STAT[:, so + 14: so + 15], ALU.mult, ALU.mult,
                (rSTAT,), (rSTAT,))
            act(K, v, v, AF.Identity, (rv, rSTAT), (rv,), bias=STAT[:, so + 15: so + 16], scale=STAT[:, so + 14: so + 15])
            tt(K, "pool", v, v, BCLG[:], ALU.mult, (rv, rBC), (rv,))
            tt(K, "pool", X[:, i, :], v, BCLB[:], ALU.add, (rv, rBC), (rX[i],))

        def epilogue_b(i, s, nxt, last):
            if last:
                K.dma("sp", out_d[s, i * 128:(i + 1) * 128, :], X[:, i, :], (rX[i],), ())
            else:
                transpose_mod(i, nxt[0], nxt[1], s)

        EPS_LN = sb("EPSC", [128, 4], F32)
        rEPS = Res()
        mset(K, "dve", EPS_LN[:, 0:1], LN_EPS, (rSTAT,))
        mset(K, "dve", EPS_LN[:, 1:2], RMS_EPS, (rSTAT,))
        mset(K, "dve", EPS_LN[:, 2:3], 1.0, (rSTAT,))
        mset(K, "dve", EPS_LN[:, 3:4], 0.0, (rSTAT,))

        def ffn(l, s, nxt, last):
            ph_reset()
            HC = NFC // 2
            UT = ph(HC * S, BF16).rearrange("p (c t) -> p c t", c=HC)
            WD = ph(HC * D, BF16).rearrange("p (c n) -> p c n", c=HC)
            WU = [ph(2048, BF16).rearrange("p (k n) -> p k n", k=8) for k in range(2)]
            rUT = [Res() for _ in range(HC)]
            rWD = Res()
            rWUa = [Res(), Res()]
            rWUg = [Res(), Res()]
            AB = [ph(516, F32), ph(516, F32)]
            TT_ = [ph(512, F32), ph(512, F32)]
            rAB = [Res(), Res()]
            rTT = [Res(), Res()]
            CW = ph(NFC * 3, F32).rearrange("p (c t) -> p c t", t=3)
            CBI = ph(NFC, F32)
            rCW = Res()
            VT2[0] = ph(D, F32)
            rVT[1] = Res()
            K.dma("sp", CW, fcw_d[l], (), (rCW,))
            K.dma("sp", CBI, fcb_d[l], (), (rCW,))
            setup_epilogue(l, 1, s)
            it = 0
            it2 = 0
            for half in range(2):
                K.dma("pool", WD, wdn_d[l][half * HC * 128:(half + 1) * HC * 128, :].rearrange("(c p) n -> p c n", p=128),
                      (), (rWD,))
                for cl in range(HC):
                    c = half * HC + cl
                    slot = it % 2
                    it += 1
                    wu = WU[slot]
                    K.dma("pool", wu[:, :, 0:128], wup_d[l][:, c * 128:(c + 1) * 128].rearrange("(k p) n -> p k n", p=128),
                          (), (rWUa[slot],))
                    K.dma("pool", wu[:, :, 128:256],
                          wup_d[l][:, DFF + c * 128:DFF + (c + 1) * 128].rearrange("(k p) n -> p k n", p=128),
                          (), (rWUg[slot],))
                    for r in range(NR):
                        t0 = r * 512
                        s2 = it2 % 2
                        s3 = it2 % 3
                        it2 += 1
                        ba, bg = s3, 3 + s3
                        hts = [rHT[r * 4 + j] for j in range(4)]
                        for k in range(8):
                            mm(K, PSB[ba][:, :], wu[:, k, 0:128], HT[:, k, t0:t0 + 512], k == 0, k == 7,
                               [rWUa[slot]] + hts, (rP[ba],))
                        for k in range(8):
                            mm(K, PSB[bg][:, :], wu[:, k, 128:256], HT[:, k, t0:t0 + 512], k == 0, k == 7,
                               [rWUg[slot]] + hts, (rP[bg],))
                        ab, rab = AB[s2], rAB[s2]
                        if r == 0:
                            mset(K, "pool", ab[:, 0:2], 0.0, (rab,))
                        else:
                            cpy(K, "pool", ab[:, 0:2], AB[1 - s2][:, 512:514], (rAB[1 - s2],), (rab,))
                        cpy(K, "act", ab[:, 2:514], PSB[ba][:, :], (rP[ba],), (rab,))
                        T = TT_[s2]
                        rT = rTT[s2]
                        ts(K, "dve", T, ab[:, 2:514], CW[:, c, 2:3], CBI[:, c:c + 1], ALU.mult, ALU.add, (rab, rCW), (rT,))
                        stt(K, T, ab[:, 1:513], CW[:, c, 1:2], T, ALU.mult, ALU.add, (rab, rCW, rT), (rT,))
                        stt(K, T, ab[:, 0:512], CW[:, c, 0:1], T, ALU.mult, ALU.add, (rab, rCW, rT), (rT,))
                        act(K, T, T, AF.Silu, (rT,), (rT,))
                        tt(K, "dve", UT[:, cl, t0:t0 + 512], T, PSB[bg][:, :], ALU.mult, (rT, rP[bg]), (rUT[cl],))
                YB = [(4, 5), (6, 7)] if half == 0 else [(4, 5), (2, 3)]
                for i2 in range(NT + 2):
                    if i2 < NT:
                        i = i2
                        yb = YB[i % 2]
                        for h in range(2):
                            b = yb[h]
                            for cl in range(HC):
                                mm(K, PSB[b][:, :], UT[:, cl, i * 128:(i + 1) * 128], WD[:, cl, h * 512:(h + 1) * 512],
                                   cl == 0, cl == HC - 1, (rUT[cl], rWD), (rP[b],))
                    if 1 <= i2 <= NT:
                        i = i2 - 1
                        yb = YB[i % 2]
                        if half == 0:
                            v = VT[:, 0, :] if i % 2 == 0 else VT2[0]
                            rv = rVT[i % 2]
                            for h in range(2):
                                tt(K, "dve", v[:, h * 512:(h + 1) * 512], PSB[yb[h]][:, :], BCG[:, h * 512:(h + 1) * 512],
                                   ALU.mult, (rP[yb[h]], rBC), (rv,))
                            stt(K, X[:, i, :], X[:, i, :], ALPHA, v, ALU.mult, ALU.add, (rX[i], rv), (rX[i],))
                        else:
                            epilogue(i, yb, s, nxt, last, alpha=1.0)
                    if i2 >= 2 and half == 1:
                        epilogue_b(i2 - 2, s, nxt, last)

        def run_pipeline(units, nstage, on=True):
            n = len(units)
            if not on:
                for u in units:
                    for st_ in u:
                        st_()
                return
            for t in range(n + nstage - 1):
                for sidx in range(nstage):
                    u = t - sidx
                    if 0 <= u < n:
                        units[u][sidx]()

        def _sb_unit(g, hp, r, hh, b_, sl, first, evac, QT, KT, V, YT, E32, SPB, ATB, RB, rQ, rK, rV, rR, rE, rSP, rAT, rYT):
            NSL = len(E32)
            ub = sl
            q0 = r * 512
            p0 = hh * 64
            kb0 = b_ * 128
            c0 = max(0, kb0 - q0)
            ncol = 512 - c0
            diag = kb0 >= q0
            P0, P1 = ub % 2, 2 + ub % 2
            lhsK = KT[p0:p0 + 64, hp, kb0:kb0 + 128]
            rhsQ = QT[p0:p0 + 64, hp, q0 + c0:q0 + 512]

            def stA():
                mm(K, PSB[P0][:, 0:ncol], lhsK, rhsQ, True, True, (rK, rQ), (rP[P0],))
                act(K, E32[sl][:, 0:ncol], PSB[P0][:, 0:ncol], AF.Exp, (rP[P0],), (rE[sl],))
                act(K, SPB[sl][:, 0:ncol], E32[sl][:, 0:ncol], AF.Ln, (rE[sl], rSTAT), (rSP[sl],), bias=EPS_LN[:, 2:3])
                if diag:
                    tt(K, "pool", SPB[sl][:, 0:128], SPB[sl][:, 0:128], TRI, ALU.mult, (rSP[sl], rC), (rSP[sl],))

            def stB():
                if first:
                    mset(K, "pool", RB, 0.0, (rR,))
                mm(K, PSB[P1][:, 0:ncol], NEGU, SPB[sl][:, 0:ncol], True, first, (rC, rSP[sl]), (rP[P1],))
                if not first:
                    mm(K, PSB[P1][:, 0:ncol], NEGONES, RB[:, c0:512], False, True, (rC, rR), (rP[P1],))
                act(K, ATB[sl][:, 0:ncol], PSB[P1][:, 0:ncol], AF.Exp, (rP[P1],), (rAT[sl],))
                tt(K, "dve", ATB[sl][:, 0:ncol], E32[sl][:, 0:ncol], ATB[sl][:, 0:ncol], ALU.mult, (rE[sl], rAT[sl]), (rAT[sl],))
                if diag:
                    tt(K, "pool", ATB[sl][:, 0:128], ATB[sl][:, 0:128], TRI, ALU.mult, (rAT[sl], rC), (rAT[sl],))
                if b_ > 0:
                    tt(K, "dve", RB[:, c0:512], RB[:, c0:512], SPB[sl][:, 0:ncol], ALU.add, (rR, rSP[sl]), (rR,))

            def stC():
                mm(K, PSB[6][p0:p0 + 64, c0:512], V[:, b_, (hp * 2 + hh) * 64:(hp * 2 + hh + 1) * 64],
                   ATB[sl][:, 0:ncol], first, b_ == 0, (rV, rAT[sl]), (rP[6],))
                if evac:
                    cpy(K, "act", YT[:, g * 2 + hp, q0:q0 + 512], PSB[6][:, :], (rP[6],), (rYT[r],))

            return (stA, (lambda: None), stB, stC)

        def _mla_unit(hp, r, hh, b_, nb, u, QB, KB, VB, PT, RC, YT, rQB, rKB, rVB, rPT, rRC, sc_mla):
            q0 = r * 512
            p0 = hh * 64
            kb0 = b_ * 128
            c0 = max(0, kb0 - q0)
            ncol = 512 - c0
            z = 4 + u % 2
            sl = u % len(PT)

            def stA():
                mm(K, PSB[z][:, 0:ncol], KB[0:96, hh, kb0:kb0 + 128], QB[0:96, hh, q0 + c0:q0 + 512], True, True,
                   (rKB, rQB), (rP[z],))
                act(K, PT[sl][:, 0:ncol], PSB[z][:, 0:ncol], AF.Exp, (rP[z],), (rPT[sl],), scale=sc_mla)
                if kb0 >= q0:
                    mset(K, "pool", PT[sl][64:128, 0:64], 0.0, (rPT[sl],))

            def stB():
                mm(K, PSB[6][p0:p0 + 64, c0:512], VB[:, b_, hh * 64:(hh + 1) * 64], PT[sl][:, 0:ncol],
                   b_ == 0, b_ == nb - 1, (rVB, rPT[sl]), (rP[6],))
                mm(K, PSB[7][p0:p0 + 64, c0:512], ONESB[:, 0:64], PT[sl][:, 0:ncol],
                   b_ == 0, b_ == nb - 1, (rC, rPT[sl]), (rP[7],))
                if hh == 1 and b_ == nb - 1:
                    recip(K, RC, PSB[7][:, :], (rP[7],), (rRC,))
                    tt(K, "dve", YT[:, 4 + hp, q0:q0 + 512], PSB[6][:, :], RC, ALU.mult, (rP[6], rRC),
                       [rHT[r * 4 + q] for q in range(4)])

            return (stA, stB)

        def _sb_unit2(g, hp, r, b_, u, first, evac, QT, KT, V, YT, E32, SPB, ATB, RB, rQ, rK, rV, rR, rE, rSP, rAT, rYT):
            sl = u % 2
            q0 = r * 512
            kb0 = b_ * 128
            c0 = max(0, kb0 - q0)
            ncol = 512 - c0
            diag = kb0 >= q0
            pa = u % 2
            P0v = PS2[pa].rearrange("p (h t) -> p h t", h=2)
            P1v = PS2[2].rearrange("p (h t) -> p h t", h=2)
            rP0 = (rP[2 * pa], rP[2 * pa + 1])
            rP1 = (rP[4], rP[5])
            TRIB = TRI.unsqueeze(1).to_broadcast([128, 2, 128])

            def stA():
                for hh in range(2):
                    p0 = hh * 64
                    mm(K, P0v[:, hh, 0:ncol], KT[p0:p0 + 64, hp, kb0:kb0 + 128], QT[p0:p0 + 64, hp, q0 + c0:q0 + 512],
                       True, True, (rK, rQ), (rP0[hh],))
                act(K, E32[sl][:, :, 0:ncol], P0v[:, :, 0:ncol], AF.Exp, rP0, (rE[sl],))
                act(K, SPB[sl][:, :, 0:ncol], E32[sl][:, :, 0:ncol], AF.Ln, (rE[sl],), (rSP[sl],), bias=1.0)
                if diag:
                    tt(K, "pool", SPB[sl][:, :, 0:128], SPB[sl][:, :, 0:128], TRIB, ALU.mult, (rSP[sl], rC), (rSP[sl],))

            def stB():
                if first:
                    mset(K, "pool", RB, 0.0, (rR,))
                for hh in range(2):
                    mm(K, P1v[:, hh, 0:ncol], NEGU, SPB[sl][:, hh, 0:ncol], True, first, (rC, rSP[sl]), (rP1[hh],))
                    if not first:
                        mm(K, P1v[:, hh, 0:ncol], NEGONES, RB[:, hh, c0:512], False, True, (rC, rR), (rP1[hh],))
                act(K, ATB[sl][:, :, 0:ncol], P1v[:, :, 0:ncol], AF.Exp, rP1, (rAT[sl],))
                tt(K, "dve", ATB[sl][:, :, 0:ncol], E32[sl][:, :, 0:ncol], ATB[sl][:, :, 0:ncol], ALU.mult,
                   (rE[sl], rAT[sl]), (rAT[sl],))
                if diag:
                    tt(K, "pool", ATB[sl][:, :, 0:128], ATB[sl][:, :, 0:128], TRIB, ALU.mult, (rAT[sl], rC), (rAT[sl],))
                if b_ > 0:
                    tt(K, "dve", RB[:, :, c0:512], RB[:, :, c0:512], SPB[sl][:, :, 0:ncol], ALU.add, (rR, rSP[sl]), (rR,))

            def stC():
                for hh in range(2):
                    p0 = hh * 64
                    mm(K, PSB[6][p0:p0 + 64, c0:512], V[:, b_, (hp * 2 + hh) * 64:(hp * 2 + hh + 1) * 64],
                       ATB[sl][:, hh, 0:ncol], first, b_ == 0, (rV, rAT[sl]), (rP[6],))
                if evac:
                    cpy(K, "act", YT[:, g * 2 + hp, q0:q0 + 512], PSB[6][:, :], (rP[6],), (rYT[r],))

            return (stA, stB, stC)

        def odd_mixer(l, s):
            io = sum(1 for k in kinds[:l] if k == "odd")
            ph_reset()
            YT = ph(8 * S, BF16).rearrange("p (c t) -> p c t", c=8)
            rYT = [Res() for _ in range(NR)]
            mark = ph_off[0]
            W3 = [ph(2048, BF16).rearrange("p (k n) -> p k n", k=8) for _ in range(3)]
            rW3 = [Res(), Res(), Res()]
            QT = ph(2 * S, BF16).rearrange("p (c t) -> p c t", c=2)
            KT = ph(2 * S, BF16).rearrange("p (c t) -> p c t", c=2)
            V = ph(NT * 256, BF16).rearrange("p (i n) -> p i n", i=NT)
            rQ, rK, rV, rR = Res(), Res(), Res(), Res()
            NSL = 2
            E32 = [ph(1024, F32).rearrange("p (h t) -> p h t", h=2) for _ in range(NSL)]
            SPB = [ph(1024, BF16).rearrange("p (h t) -> p h t", h=2) for _ in range(NSL)]
            ATB = [ph(1024, BF16).rearrange("p (h t) -> p h t", h=2) for _ in range(NSL)]
            RB = ph(1024, BF16).rearrange("p (h t) -> p h t", h=2)
            rE = [Res() for _ in range(NSL)]
            rSP = [Res() for _ in range(NSL)]
            rAT = [Res() for _ in range(NSL)]
            setup_epilogue(l, 0, s)
            pb = 0
            unit = 0
            def load_w3(g):
                for j in range(3):
                    K.dma("pool", W3[j], oqkv_d[io][:, j * D + g * 256: j * D + (g + 1) * 256].rearrange("(k p) n -> p k n", p=128),
                          (), (rW3[j],))

            load_w3(0)
            for g in range(4):
                for dst, rdst, wi, scale in ((QT, rQ, 0, 0.125), (KT, rK, 1, 1.0)):
                    for j in range(2):
                        for r in range(NR):
                            b = pb % 2
                            pb += 1
                            hts = [rHT[r * 4 + q] for q in range(4)]
                            for k in range(8):
                                mm(K, PSB[b][:, :], W3[wi][:, k, j * 128:(j + 1) * 128], HT[:, k, r * 512:(r + 1) * 512],
                                   k == 0, k == 7, [rW3[wi]] + hts, (rP[b],))
                            act(K, dst[:, j, r * 512:(r + 1) * 512], PSB[b][:, :], AF.Copy, (rP[b],), (rdst,), scale=scale)
                for i in range(NT):
                    b = pb % 2
                    pb += 1
                    for k in range(8):
                        mm(K, PSB[b][:, 0:256], HT[:, k, i * 128:(i + 1) * 128], W3[2][:, k, :], k == 0, k == 7,
                           (rW3[2], rHT[i]), (rP[b],))
                    cpy(K, "dve", V[:, i, :], PSB[b][:, 0:256], (rP[b],), (rV,))
                units = []
                for hp in range(2):
                    for r in range(NR):
                        nb = 4 * (r + 1)
                        first = True
                        for b_ in reversed(range(nb)):
                            units.append(_sb_unit2(g, hp, r, b_, unit, first, b_ == 0,
                                                   QT, KT, V, YT, E32, SPB, ATB, RB, rQ, rK, rV, rR, rE, rSP, rAT, rYT))
                            unit += 1
                            first = False
                if g + 1 < 4:
                    load_w3(g + 1)
                run_pipeline(units, 3, PIPE_ODD)
            ph_reset(mark)
            WO = ph(8 * D, BF16).rearrange("p (k n) -> p k n", k=8)
            rWO = Res()
            VT2[0] = ph(D, F32)
            rVT[1] = Res()
            rWOh = [Res(), Res()]
            for h in range(2):
                K.dma("pool", WO[:, :, h * 512:(h + 1) * 512],
                      owo_d[io][:, h * 512:(h + 1) * 512].rearrange("(k p) n -> p k n", p=128), (), (rWOh[h],))
            YB = [(4, 5), (2, 3)]
            for i in range(NT + 2):
                if i < NT:
                    yb = YB[i % 2]
                    for h in range(2):
                        for k in range(8):
                            mm(K, PSB[yb[h]][:, :], YT[:, k, i * 128:(i + 1) * 128], WO[:, k, h * 512:(h + 1) * 512],
                               k == 0, k == 7, (rYT[i // 4], rWOh[h]), (rP[yb[h]],))
                if 1 <= i <= NT:
                    epilogue(i - 1, YB[(i - 1) % 2], s, (l, 1), False)
                if i >= 2:
                    epilogue_b(i - 2, s, (l, 1), False)


        def make_tables(s, dcos, dsin, FREQ, SGN, rT, Ta, Tb, Tc, Ti, rTmp):
            NC = Ta.shape[1]
            for c0 in range(0, S, NC):
                K.dma("sp", Ti, pos_d[s, c0:c0 + NC].partition_broadcast(128), (), (rTmp,))
                cpy(K, "dve", Ta, Ti, (rTmp,), (rTmp,))
                ts(K, "dve", Ta, Ta, FREQ, None, ALU.mult, None, (rTmp, rC), (rTmp,))
                for dst, phase in ((dsin, 0.0), (dcos, 0.25)):
                    ts(K, "dve", Tb, Ta, 1.0 / TWO_PI, phase, ALU.mult, ALU.add, (rTmp,), (rTmp,))
                    cpy(K, "dve", Ti, Tb, (rTmp,), (rTmp,))
                    cpy(K, "dve", Tb, Ti, (rTmp,), (rTmp,))
                    ts(K, "dve", Tb, Tb, -TWO_PI, phase * TWO_PI, ALU.mult, ALU.add, (rTmp,), (rTmp,))
                    tt(K, "dve", Tb, Tb, Ta, ALU.add, (rTmp,), (rTmp,))
                    ts(K, "dve", Tc, Tb, math.pi, TWO_PI, ALU.is_gt, ALU.mult, (rTmp,), (rTmp,))
                    tt(K, "dve", Tb, Tb, Tc, ALU.subtract, (rTmp,), (rTmp,))
                    ts(K, "dve", Tc, Tb, -math.pi, TWO_PI, ALU.is_lt, ALU.mult, (rTmp,), (rTmp,))
                    tt(K, "dve", Tb, Tb, Tc, ALU.add, (rTmp,), (rTmp,))
                    ts(K, "dve", Tb, Tb, math.pi, -math.pi, ALU.min, ALU.max, (rTmp,), (rTmp,))
                    act(K, Tb, Tb, AF.Sin, (rTmp,), (rTmp,))
                    if phase == 0.0:
                        ts(K, "dve", dst[:, c0:c0 + NC], Tb, SGN, None, ALU.mult, None, (rTmp, rC), (rT,))
                    else:
                        cpy(K, "dve", dst[:, c0:c0 + NC], Tb, (rTmp,), (rT,))

        def even_mixer(l, s):
            ie = sum(1 for k in kinds[:l] if k == "even")
            ph_reset()
            YT = HT
            EVS = ph(16, F32)
            rEVS = Res()
            K.dma("sp", EVS, evs_d[ie], (), (rEVS,))
            CQN = ph(3 * S, BF16).rearrange("p (c t) -> p c t", c=3)
            CKVN = ph(2 * S, BF16).rearrange("p (c t) -> p c t", c=2)
            KRR = ph(S, BF16)
            rLAT = Res()
            mark_lat = ph_off[0]
            QA = ph(4 * S, BF16).rearrange("p (c t) -> p c t", c=4)
            KA2 = ph(S, BF16)
            IQ = ph(2 * S, BF16).rearrange("p (c t) -> p c t", c=2)
            IK2 = ph(S, BF16)
            VA = ph(NT * 64, BF16).rearrange("p (i n) -> p i n", i=NT)
            IW = ph(NT * 4, F32).rearrange("p (i n) -> p i n", i=NT)
            rQA, rKA, rIQ, rIK, rVA = Res(), Res(), Res(), Res(), Res()
            mark_dsa = ph_off[0]
            TC_ = ph(S, BF16)
            TS_ = ph(S, BF16)
            rTAB = Res()
            rTAB2 = Res()
            WP = [ph(1024, BF16).rearrange("p (k n) -> p k n", k=8) for _ in range(3)]
            rWP = [Res(), Res(), Res()]
            SCR = ph(2048, F32)
            SS = [SCR[:, k * 512:(k + 1) * 512] for k in range(4)]
            rS = [Res() for _ in range(4)]
            SQB = ph(3 * 256, BF16).rearrange("p (c t) -> p c t", c=3)
            rSQ = Res()
            setup_epilogue(l, 0, s)
            K.dma("sp", TC_, tabs_d[s, 0], (rTABD[s],), (rTAB,))
            K.dma("act", TS_, tabs_d[s, 1], (rTABD[s],), (rTAB2,))
            pbk = [0]

            def proj(blk, slot, r0, n, bank):
                hts = [rHT[q] for q in range(r0 // 128, (r0 + n + 127) // 128)]
                for k in range(8):
                    mm(K, PSB[bank][:, 0:n], WP[slot][:, k, :], HT[:, k, r0:r0 + n], k == 0, k == 7,
                       [rWP[slot]] + hts, (rP[bank],))

            def loadw(blk, slot):
                K.dma("pool", WP[slot], win_d[ie][:, blk * 128:(blk + 1) * 128].rearrange("(k p) n -> p k n", p=128),
                      (), (rWP[slot],))

            def rope_out(dst, rdst, bA, bB, r0, p_lo, p_hi, TCt, TSt):
                t1, t2 = SS[0], SS[1]
                tt(K, "dve", t1[p_lo:p_hi, :], PSB[bA][p_lo:p_hi, :], TCt[p_lo:p_hi, r0:r0 + 512], ALU.mult,
                   (rP[bA], rTAB), (rS[0],))
                tt(K, "dve", t2[p_lo:p_hi, :], PSB[bB][p_lo:p_hi, :], TSt[p_lo:p_hi, r0:r0 + 512], ALU.mult,
                   (rP[bB], rTAB2), (rS[1],))
                tt(K, "pool", dst, t1[p_lo:p_hi, :], t2[p_lo:p_hi, :], ALU.add, (rS[0], rS[1]), (rdst,))

            ropes = [(j, 4 + j, (lambda r0, j=j: QA[:, j, r0:r0 + 512]), rQA) for j in range(4)]
            ropes.append((8, 9, (lambda r0: KA2[:, r0:r0 + 512]), rKA))
            ropes += [(10 + j, 12 + j, (lambda r0, j=j: IQ[:, j, r0:r0 + 512]), rIQ) for j in range(2)]
            for bA_, bB_, dstf, rd in ropes:
                loadw(bA_, 0)
                loadw(bB_, 1)
                for r in range(NR):
                    b0 = (pbk[0] % 2) * 2
                    pbk[0] += 1
                    proj(bA_, 0, r * 512, 512, b0)
                    proj(bB_, 1, r * 512, 512, b0 + 1)
                    rope_out(dstf(r * 512), rd, b0, b0 + 1, r * 512, 0, 128, TC_, TS_)
            loadw(14, 0)
            loadw(15, 1)
            for r in range(NR):
                r0 = r * 512
                b0 = (pbk[0] % 2) * 2
                pbk[0] += 1
                proj(14, 0, r0, 512, b0)
                proj(15, 1, r0, 512, b0 + 1)
                cpy(K, "act", SS[0], PSB[b0][:, :], (rP[b0],), (rS[0],))
                cpy(K, "act", SS[1], PSB[b0 + 1][:, :], (rP[b0 + 1],), (rS[1],))
                mm(K, PSB[b0][:, :], BLK64, SS[0], True, True, (rC, rS[0]), (rP[b0],))
                tt(K, "dve", SS[0], SS[0], PSB[b0][:, :], ALU.subtract, (rS[0], rP[b0]), (rS[0],))
                tt(K, "dve", SS[1], SS[1], PSB[b0][:, :], ALU.subtract, (rS[1], rP[b0]), (rS[1],))
                act(K, SS[2], SS[0], AF.Square, (rS[0],), (rS[2],))
                mm(K, PSB[b0 + 1][:, :], BLK64, SS[2], True, True, (rC, rS[2]), (rP[b0 + 1],))
                act(K, SS[2], PSB[b0 + 1][:, :], AF.Sqrt, (rP[b0 + 1], rSTAT), (rS[2],), bias=EPS_LN[:, 0:1])
                recip(K, SS[2], SS[2], (rS[2],), (rS[2],))
                tt(K, "dve", SS[0], SS[0], SS[2], ALU.mult, (rS[0], rS[2]), (rS[0],))
                tt(K, "dve", SS[1], SS[1], SS[2], ALU.mult, (rS[1], rS[2]), (rS[1],))
                ts(K, "dve", SS[0], SS[0], EVS[:, 0:1], EVS[:, 1:2], ALU.mult, ALU.add, (rS[0], rEVS), (rS[0],))
                ts(K, "dve", SS[1], SS[1], EVS[:, 2:3], EVS[:, 3:4], ALU.mult, ALU.add, (rS[1], rEVS), (rS[1],))
                tt(K, "dve", SS[0], SS[0], TC_[:, r0:r0 + 512], ALU.mult, (rS[0], rTAB), (rS[0],))
                tt(K, "dve", SS[1], SS[1], TS_[:, r0:r0 + 512], ALU.mult, (rS[1], rTAB2), (rS[1],))
                tt(K, "pool", IK2[:, r0:r0 + 512], SS[0], SS[1], ALU.add, (rS[0], rS[1]), (rIK,))
            for blk0, nch, dstL, gcol, dim in ((16, 3, CQN, 4, 384.0), (19, 2, CKVN, 7, 256.0)):
                TMP = SCR[:, 0:nch * 256].rearrange("p (c t) -> p c t", c=nch)
                for j in range(nch):
                    loadw(blk0 + j, j)
                for rr in range(S // 256):
                    r0 = rr * 256
                    for j in range(nch):
                        b0 = pbk[0] % 2
                        pbk[0] += 1
                        proj(blk0 + j, j, r0, 256, b0)
                        cpy(K, "act", TMP[:, j, :], PSB[b0][:, 0:256], (rP[b0],), (rS[0], rS[1]))
                        act(K, SQB[:, j, :], PSB[b0][:, 0:256], AF.Square, (rP[b0],), (rSQ,))
                    for j in range(nch):
                        mm(K, PSB[2][:, 0:256], ONESB, SQB[:, j, :], j == 0, j == nch - 1, (rC, rSQ), (rP[2],))
                    act(K, SS[3][:, 0:256], PSB[2][:, 0:256], AF.Sqrt, (rP[2], rSTAT), (rS[3],), bias=EPS_LN[:, 1:2],
                        scale=1.0 / dim)
                    recip(K, SS[3][:, 0:256], SS[3][:, 0:256], (rS[3],), (rS[3],))
                    for j in range(nch):
                        stt(K, dstL[:, j, r0:r0 + 256], TMP[:, j, :], EVS[:, gcol + j:gcol + j + 1], SS[3][:, 0:256],
                            ALU.mult, ALU.mult, (rS[0], rS[1], rEVS, rS[3]), (rLAT,))
            loadw(23, 0)
            for i in range(NT):
                b0 = pbk[0] % 2
                pbk[0] += 1
                for k in range(8):
                    mm(K, PSB[b0][:, 0:128], HT[:, k, i * 128:(i + 1) * 128], WP[0][:, k, :], k == 0, k == 7,
                       (rWP[0], rHT[i]), (rP[b0],))
                cpy(K, "act", VA[:, i, :], PSB[b0][:, 0:64], (rP[b0],), (rVA,))
                ts(K, "dve", IW[:, i, :], PSB[b0][:, 64:68], 1.0 / 16.0, None, ALU.mult, None, (rP[b0],), (rVA,))
            K.dma("sp", TC_, tabs_d[s, 2], (rTABD[s],), (rTAB,))
            K.dma("act", TS_, tabs_d[s, 3], (rTABD[s],), (rTAB2,))
            loadw(21, 0)
            loadw(22, 1)
            for r in range(NR):
                b0 = (pbk[0] % 2) * 2
                pbk[0] += 1
                proj(21, 0, r * 512, 512, b0)
                proj(22, 1, r * 512, 512, b0 + 1)
                rope_out(KRR[64:96, r * 512:(r + 1) * 512], rLAT, b0, b0 + 1, r * 512, 64, 96, TC_, TS_)

            if DBG_MODE == 3:
                K.barrier()
                ov = out_d[s].rearrange("s d -> (s d)").rearrange("(c p t) -> p c t", c=8, p=128)
                dl = [QA[:, 0, 0:512], KA2[:, 0:512], IQ[:, 0, 0:512], IK2[:, 0:512], CQN[:, 0, 0:512], KRR[:, 0:512]]
                for c, a in enumerate(dl):
                    cpy(K, "dve", VT[:, 0, 0:512], a, (rHT[0],), (rVT[0],))
                    K.dma("sp", ov[:, c, 0:512], VT[:, 0, 0:512], (rVT[0],), ())
                mset(K, "dve", VT[:, 0, 0:512], 0.0, (rVT[0],))
                cpy(K, "dve", VT[:, 0, 0:256], VA[:, 0:4, :], (rHT[0],), (rVT[0],))
                cpy(K, "dve", VT[:, 0, 256:272], IW[:, 0:4, :], (rHT[0],), (rVT[0],))
                K.dma("sp", ov[:, 6, 0:512], VT[:, 0, 0:512], (rVT[0],), ())
                return
            ph_reset(mark_dsa)
            pairs = [(a, NT - 1 - a) for a in range(NT // 2)]
            PW = (NT + 1) * 128
            SCP = ph(PW, F32)
            MBP = [ph(PW, BF16), ph(PW, BF16)]
            RLS = [ph(512, F32) for _ in range(3)]
            PT = [ph(512, BF16) for _ in range(3)]
            RC = ph(512, F32)
            M8 = [ph(8, F32), ph(8, F32)]
            rSCt = [Res(), Res()]
            rMBt = [[Res(), Res()], [Res(), Res()]]
            rRLS = [Res() for _ in range(3)]
            rRC = Res()
            rM8 = [Res(), Res()]
            rPT = [Res() for _ in range(3)]
            cnt = {"ib": 0, "un": 0}

            def tile_view(buf, pr, w):
                a, b = pr
                off = 0 if w == 0 else (a + 1) * 128
                i = pr[w]
                return buf[:, off:off + (i + 1) * 128]

            def indexer(pr):
                for w in range(2):
                    i = pr[w]
                    t0 = i * 128
                    SK = (i + 1) * 128
                    SC = tile_view(SCP, pr, w)
                    for hh in range(4):
                        jh, p0 = hh // 2, (hh % 2) * 64
                        for k0 in range(0, SK, 512):
                            kn = min(512, SK - k0)
                            b = cnt["ib"] % 2
                            RL = RLS[cnt["ib"] % 3]
                            rRL = rRLS[cnt["ib"] % 3]
                            cnt["ib"] += 1
                            mm(K, PSB[b][:, 0:kn], IQ[p0:p0 + 64, jh, t0:t0 + 128], IK2[p0:p0 + 64, k0:k0 + kn], True, True,
                               (rIQ, rIK), (rP[b],))
                            if hh == 0:
                                act(K, RL[:, 0:kn], PSB[b][:, 0:kn], AF.Relu, (rP[b],), (rRL,))
                                ts(K, "dve", SC[:, k0:k0 + kn], RL[:, 0:kn], IW[:, i, 0:1], None, ALU.mult, None,
                                   (rRL, rVA), (rSCt[w],))
                            else:
                                act(K, RL[:, 0:kn], PSB[b][:, 0:kn], AF.Relu, (rP[b],), (rRL,))
                                stt(K, SC[:, k0:k0 + kn], RL[:, 0:kn], IW[:, i, hh:hh + 1], SC[:, k0:k0 + kn], ALU.mult, ALU.add,
                                    (rRL, rVA, rSCt[w]), (rSCt[w],))
                    mset(K, "dve", SC[0:64, t0 + 64:t0 + 128], NEG, (rSCt[w],))

            def topk_rounds(pr):
                ws = [w for w in range(2) if (2 * pr[w] + 1) * 64 > TOPK]
                rounds = []
                for rd in range(TOPK // 8):
                    def one(ws=ws):
                        for w in ws:
                            _max8(tile_view(SCP, pr, w), M8[w], rSCt[w], rM8[w])
                        for w in ws:
                            _mrep(tile_view(SCP, pr, w), M8[w], rSCt[w], rM8[w])
                    if ws:
                        rounds.append(one)
                return rounds

            def _max8(SC, m8, rsc, rm8):
                K.op("dve", lambda e: e.max(out=m8, in_=SC), (rsc,), (rm8,))

            def _mrep(SC, m8, rsc, rm8):
                K.op("dve", lambda e: e.match_replace(out=SC, in_to_replace=m8, in_values=SC, imm_value=NEG2), (rsc, rm8), (rsc,))

            def masks(pr, par):
                for w in range(2):
                    SC = tile_view(SCP, pr, w)
                    MBt = tile_view(MBP[par], pr, w)
                    if (2 * pr[w] + 1) * 64 > TOPK:
                        ts(K, "dve", MBt, SC, -2.0e30, MASKV, ALU.is_ge, ALU.mult, (rSCt[w],), (rMBt[par][w],))
                    else:
                        ts(K, "dve", MBt, SC, -1.0e29, MASKV, ALU.is_lt, ALU.mult, (rSCt[w],), (rMBt[par][w],))

            def att_unit(pr, par, w, hs, b_, nb):
                i = pr[w]
                t0 = i * 128
                p0 = hs * 64
                kb0 = b_ * 128
                u = cnt["un"]
                cnt["un"] += 1
                pl = 4 + u % 2
                sl = u % 3
                MBt = tile_view(MBP[par], pr, w)

                def stA():
                    mm(K, PSB[pl][:, :], KA2[p0:p0 + 64, kb0:kb0 + 128], QA[p0:p0 + 64, :, t0:t0 + 128], True, False,
                       (rKA, rQA), (rP[pl],))
                    mm(K, PSB[pl][:, :], MBt[:, kb0:kb0 + 128], IREP, False, True, (rMBt[par][w], rC), (rP[pl],))
                    act(K, PT[sl], PSB[pl][:, :], AF.Exp, (rP[pl],), (rPT[sl],), scale=0.125)

                po, pd = (6, 7) if w == 0 else (2, 3)

                def stB():
                    mm(K, PSB[po][p0:p0 + 64, :], VA[:, b_, :], PT[sl], b_ == 0, b_ == nb - 1, (rVA, rPT[sl]), (rP[po],))
                    mm(K, PSB[pd][p0:p0 + 64, :], ONESB[:, 0:64], PT[sl], b_ == 0, b_ == nb - 1, (rC, rPT[sl]), (rP[pd],))
                    if hs == 1 and b_ == nb - 1:
                        recip(K, RC, PSB[pd][:, :], (rP[pd],), (rRC,))
                        tt(K, "dve", YT[:, 0:4, t0:t0 + 128], PSB[po][:, :].rearrange("p (h t) -> p h t", h=4),
                           RC.rearrange("p (h t) -> p h t", h=4), ALU.mult, (rP[po], rRC), (rHT[i],))

                return (stA, stB)

            def att_units(pr, par):
                us = []
                for w in range(2):
                    nb = pr[w] + 1
                    for hs in range(2):
                        for b_ in range(nb):
                            us.append(att_unit(pr, par, w, hs, b_, nb))
                return us

            indexer(pairs[0])
            for rd in topk_rounds(pairs[0]):
                rd()
            masks(pairs[0], 0)
            for pi_, pr in enumerate(pairs):
                nxt = pairs[pi_ + 1] if pi_ + 1 < len(pairs) else None
                if nxt is not None:
                    indexer(nxt)
                us = att_units(pr, pi_ % 2)
                rounds = topk_rounds(nxt) if nxt is not None else []
                n = len(us)
                steps = n + 1
                ri = 0
                if not ILV_DSA:
                    for rd in rounds:
                        rd()
                    rounds = []
                for t in range(steps):
                    if PIPE_DSA:
                        for sidx in range(2):
                            u = t - sidx
                            if 0 <= u < n:
                                us[u][sidx]()
                    elif t < n:
                        us[t][0]()
                        us[t][1]()
                    target = (t + 1) * len(rounds) // steps
                    while ri < target:
                        rounds[ri]()
                        ri += 1
                if nxt is not None:
                    masks(nxt, (pi_ + 1) % 2)

            ph_reset(mark_lat)
            TC_ = ph(S, BF16)
            TS_ = ph(S, BF16)
            rTAB = Res()
            rTAB2 = Res()
            K.dma("sp", TC_, tabs_d[s, 2], (rTABD[s],), (rTAB,))
            K.dma("act", TS_, tabs_d[s, 3], (rTABD[s],), (rTAB2,))
            QBs = [ph(2 * S, BF16).rearrange("p (c t) -> p c t", c=2) for _ in range(2)]
            KBs = [ph(2 * S, BF16).rearrange("p (c t) -> p c t", c=2) for _ in range(2)]
            VBs = [ph(NT * 128, BF16).rearrange("p (i n) -> p i n", i=NT) for _ in range(2)]
            WUQs = [ph(3 * 192, BF16).rearrange("p (k n) -> p k n", k=3) for _ in range(2)]
            WUQSs = [ph(3 * 192, BF16).rearrange("p (k n) -> p k n", k=3) for _ in range(2)]
            WKKs = [ph(2 * 128, BF16).rearrange("p (k n) -> p k n", k=2) for _ in range(2)]
            WKVs = [ph(2 * 128, BF16).rearrange("p (k n) -> p k n", k=2) for _ in range(2)]
            T1 = ph(512, F32)
            T2 = ph(512, F32)
            PT = [ph(512, BF16) for _ in range(2)]
            RC = ph(512, F32)
            rT1, rT2, rRC = Res(), Res(), Res()
            rQBs, rKBs, rVBs = [Res(), Res()], [Res(), Res()], [Res(), Res()]
            rWUQs, rWUQSs, rWKKs, rWKVs = [Res(), Res()], [Res(), Res()], [Res(), Res()], [Res(), Res()]
            rPT = [Res() for _ in range(3)]
            sc_mla = 96.0 ** -0.5
            mcnt = {"zb": 0, "un": 0}

            def mla_w(hp):
                sl_ = hp % 2
                WUQ, WUQS, WKK, WKV = WUQs[sl_], WUQSs[sl_], WKKs[sl_], WKVs[sl_]
                rWUQ, rWUQS, rWKK, rWKV = rWUQs[sl_], rWUQSs[sl_], rWKKs[sl_], rWKVs[sl_]
                K.dma("pool", WUQ, wuq_d[ie][:, hp * 192:(hp + 1) * 192].rearrange("(k p) n -> p k n", p=128), (), (rWUQ,))
                K.dma("pool", WUQS, wuqs_d[ie][:, hp * 192:(hp + 1) * 192].rearrange("(k p) n -> p k n", p=128), (), (rWUQS,))
                K.dma("pool", WKK, wukk_d[ie][:, hp * 128:(hp + 1) * 128].rearrange("(k p) n -> p k n", p=128), (), (rWKK,))
                K.dma("pool", WKV, wukv_d[ie][:, hp * 128:(hp + 1) * 128].rearrange("(k p) n -> p k n", p=128), (), (rWKV,))

            def mla_proj(hp):
                sl_ = hp % 2
                QB, KB, VB = QBs[sl_], KBs[sl_], VBs[sl_]
                WUQ, WUQS, WKK, WKV = WUQs[sl_], WUQSs[sl_], WKKs[sl_], WKVs[sl_]
                rQB, rKB, rVB = rQBs[sl_], rKBs[sl_], rVBs[sl_]
                rWUQ, rWUQS, rWKK, rWKV = rWUQs[sl_], rWUQSs[sl_], rWKKs[sl_], rWKVs[sl_]
                for hh in range(2):
                    for r in range(NR):
                        r0 = r * 512
                        b0 = (mcnt["zb"] % 2) * 2
                        mcnt["zb"] += 1
                        for k in range(3):
                            mm(K, PSB[b0][0:96, :], WUQ[:, k, hh * 96:(hh + 1) * 96], CQN[:, k, r0:r0 + 512], k == 0, k == 2,
                               (rWUQ, rLAT), (rP[b0],))
                        for k in range(3):
                            mm(K, PSB[b0 + 1][0:96, :], WUQS[:, k, hh * 96:(hh + 1) * 96], CQN[:, k, r0:r0 + 512], k == 0, k == 2,
                               (rWUQS, rLAT), (rP[b0 + 1],))
                        cpy(K, "act", QB[0:64, hh, r0:r0 + 512], PSB[b0][0:64, :], (rP[b0],), (rQB,))
                        tt(K, "dve", T1[64:96, :], PSB[b0][64:96, :], TC_[64:96, r0:r0 + 512], ALU.mult, (rP[b0], rTAB), (rT1,))
                        tt(K, "dve", T2[64:96, :], PSB[b0 + 1][64:96, :], TS_[64:96, r0:r0 + 512], ALU.mult, (rP[b0 + 1], rTAB2), (rT2,))
                        tt(K, "pool", QB[64:96, hh, r0:r0 + 512], T1[64:96, :], T2[64:96, :], ALU.add, (rT1, rT2), (rQB,))
                    for r in range(NR):
                        r0 = r * 512
                        b0 = (mcnt["zb"] % 2) * 2
                        mcnt["zb"] += 1
                        for k in range(2):
                            mm(K, PSB[b0][0:64, :], WKK[:, k, hh * 64:(hh + 1) * 64], CKVN[:, k, r0:r0 + 512], k == 0, k == 1,
                               (rWKK, rLAT), (rP[b0],))
                        cpy(K, "act", KB[0:64, hh, r0:r0 + 512], PSB[b0][0:64, :], (rP[b0],), (rKB,))
                    cpy(K, "pool", KB[64:96, hh, :], KRR[64:96, :], (rLAT,), (rKB,))
                for i in range(NT):
                    b0 = (mcnt["zb"] % 2) * 2
                    mcnt["zb"] += 1
                    for k in range(2):
                        mm(K, PSB[b0][:, 0:128], CKVN[:, k, i * 128:(i + 1) * 128], WKV[:, k, :], k == 0, k == 1,
                           (rWKV, rLAT), (rP[b0],))
                    cpy(K, "act", VB[:, i, :], PSB[b0][:, 0:128], (rP[b0],), (rVB,))

            def mla_att(hp):
                sl_ = hp % 2
                units = []
                for r in range(NR):
                    nb = 4 * (r + 1)
                    for hh in range(2):
                        for b_ in range(nb):
                            units.append(_mla_unit(hp, r, hh, b_, nb, mcnt["un"], QBs[sl_], KBs[sl_], VBs[sl_], PT, RC, YT,
                                                   rQBs[sl_], rKBs[sl_], rVBs[sl_], rPT, rRC, sc_mla))
                            mcnt["un"] += 1
                run_pipeline(units, 2, PIPE_MLA)

            mla_w(0)
            mla_w(1)
            mla_proj(0)
            for hp in range(4):
                if hp + 1 < 4:
                    mla_proj(hp + 1)
                if hp + 2 < 4:
                    mla_w(hp + 2)
                mla_att(hp)

            if DEBUG_YT:
                K.barrier()
                ov = out_d[s].rearrange("s d -> (s d)").rearrange("(c p t) -> p c t", c=8, p=128)
                for c in range(8):
                    for t0 in range(0, S, 512):
                        cpy(K, "dve", VT[:, 0, 0:512], YT[:, c, t0:t0 + 512], (rHT[0],), (rVT[0],))
                        K.dma("sp", ov[:, c, t0:t0 + 512], VT[:, 0, 0:512], (rVT[0],), ())
                return
            ph_reset()
            WO = ph(8 * D, BF16).rearrange("p (k n) -> p k n", k=8)
            rWO = Res()
            VT2[0] = ph(D, F32)
            rVT[1] = Res()
            rWOh = [Res(), Res()]
            for h in range(2):
                K.dma("pool", WO[:, :, h * 512:(h + 1) * 512],
                      ewo_d[ie][:, h * 512:(h + 1) * 512].rearrange("(k p) n -> p k n", p=128), (), (rWOh[h],))
            YB = [(4, 5), (2, 3)]
            for i in range(NT + 2):
                if i < NT:
                    yb = YB[i % 2]
                    for h in range(2):
                        for k in range(8):
                            mm(K, PSB[yb[h]][:, :], YT[:, k, i * 128:(i + 1) * 128], WO[:, k, h * 512:(h + 1) * 512],
                               k == 0, k == 7, (rHT[i], rWOh[h]), (rP[yb[h]],))
                if 1 <= i <= NT:
                    epilogue(i - 1, YB[(i - 1) % 2], s, (l, 1), False)
                if i >= 2:
                    epilogue_b(i - 2, s, (l, 1), False)

        for s in range(NSEQ):
            for i in range(NT):
                K.dma("sp" if i % 2 == 0 else "act", X[:, i, :], x_d[s, i * 128:(i + 1) * 128, :], (), (rX[i],))
            first = (0, 0) if kinds[0] != "none" else (0, 1)
            for i in range(NT):
                transpose_mod(i, first[0], first[1], s)
            if "even" in kinds:
                ph_reset()
                NTC = min(1024, S)
                tTC, tTS = ph(S, BF16), ph(S, BF16)
                tTa, tTb, tTc = ph(NTC, F32), ph(NTC, F32), ph(NTC, F32)
                tTi = ph(NTC, F32).bitcast(I32)
                rTT_, rTmp_ = Res(), Res()
                for half, (FQ, SG) in enumerate(((FREQA, SGNA), (FREQB, SGNB))):
                    make_tables(s, tTC, tTS, FQ, SG, rTT_, tTa, tTb, tTc, tTi, rTmp_)
                    K.dma("sp", tabs_d[s, 2 * half], tTC, (rTT_,), (rTABD[s],))
                    K.dma("sp", tabs_d[s, 2 * half + 1], tTS, (rTT_,), (rTABD[s],))
            for l in range(L):
                if kinds[l] == "even":
                    even_mixer(l, s)
                elif kinds[l] == "odd":
                    odd_mixer(l, s)
                lastl = l == L - 1
                if DEBUG_YT:
                    continue
                nxt = None if lastl else ((l + 1, 0) if kinds[l + 1] != "none" else (l + 1, 1))
                ffn(l, s, nxt, lastl)
                K.flush()
        K.flush(final=True)
    return nc, K


NX_COLS = 24 * 128
DEBUG_YT = False
DBG_MODE = 0
PIPE_ODD = True
PIPE_MLA = True
PIPE_DSA = True
ILV_DSA = False


def _swap_half(w, lo, n):
    h = n // 2
    return np.concatenate([w[:, lo + h:lo + n], w[:, lo:lo + h]], axis=1)


def host_consts():
    p = np.arange(128)
    cf = np.zeros((128, 3 * 128 + 8), np.float32)
    cf[:, 0:128] = np.eye(128, dtype=np.float32)
    cf[:, 128:256] = 1.0
    cf[:, 256:384] = (p[:, None] // 64 == p[None, :] // 64).astype(np.float32) / 64.0
    inv64 = (np.float32(10000.0) ** (-np.arange(0, 64, 2, dtype=np.float32) / np.float32(64))).astype(np.float32)
    inv32 = (np.float32(10000.0) ** (-np.arange(0, 32, 2, dtype=np.float32) / np.float32(32))).astype(np.float32)
    cf[:, 384] = inv64[(p % 64) % 32]
    cf[:, 385] = np.where((p % 64) < 32, -1.0, 1.0)
    cf[:, 386] = inv32[(p % 32) % 16]
    cf[:, 387] = np.where((p % 32) < 16, -1.0, 1.0)
    cb = np.zeros((128, 4 * 128 + 512), np.float32)
    cb[:, 0:128] = 1.0
    cb[:, 128:256] = -1.0
    cb[:, 256:384] = -(p[:, None] >= p[None, :]).astype(np.float32)
    cb[:, 384:512] = (p[:, None] < p[None, :]).astype(np.float32)
    cb[:, 512:1024] = np.tile(np.eye(128, dtype=np.float32), (1, 4))
    return cf, cb


def host_prep(inp, kinds):
    L = len(kinds)
    f = lambda a: np.ascontiguousarray(np.asarray(a, dtype=np.float32))
    sh = {}
    sh["mod_w"] = f(inp["mod_w"])[:L]
    sh["mod_bT"] = f(np.asarray(inp["mod_b"])[:L].reshape(L, 48, 128).transpose(0, 2, 1))
    sh["lnp"] = f(np.stack([np.asarray(inp["ln_mix_g"])[:L], np.asarray(inp["ln_mix_b"])[:L],
                            np.asarray(inp["ln_ffn_g"])[:L], np.asarray(inp["ln_ffn_b"])[:L]], axis=1))
    ne = max(1, sum(1 for k in kinds if k == "even"))
    no = max(1, sum(1 for k in kinds if k == "odd"))
    wins, smalls, wuq, wuqs, wkk, wkv = [], [], [], [], [], []
    for i in range(ne):
        w = np.asarray(inp["ev_w_in"][i], dtype=np.float32)
        o_qa, o_ka, o_va, o_iq, o_ik, o_iw, o_cq, o_ckv, o_kr = 0, 512, 576, 640, 896, 960, 964, 1348, 1604
        blocks = []
        for j in range(4):
            blocks.append(w[:, o_qa + j * 128:o_qa + (j + 1) * 128])
        for j in range(4):
            blocks.append(np.concatenate([_swap_half(w, o_qa + (2 * j) * 64, 64), _swap_half(w, o_qa + (2 * j + 1) * 64, 64)], axis=1))
        ka = w[:, o_ka:o_ka + 64]
        kas = _swap_half(w, o_ka, 64)
        blocks += [np.concatenate([ka, ka], 1), np.concatenate([kas, kas], 1)]
        for j in range(2):
            blocks.append(w[:, o_iq + j * 128:o_iq + (j + 1) * 128])
        for j in range(2):
            blocks.append(np.concatenate([_swap_half(w, o_iq + (2 * j) * 64, 64), _swap_half(w, o_iq + (2 * j + 1) * 64, 64)], axis=1))
        ik = w[:, o_ik:o_ik + 64]
        iks = _swap_half(w, o_ik, 64)
        blocks += [np.concatenate([ik, ik], 1), np.concatenate([iks, iks], 1)]
        for j in range(3):
            blocks.append(w[:, o_cq + j * 128:o_cq + (j + 1) * 128])
        for j in range(2):
            blocks.append(w[:, o_ckv + j * 128:o_ckv + (j + 1) * 128])
        kr = w[:, o_kr:o_kr + 32]
        krs = _swap_half(w, o_kr, 32)
        blocks += [np.concatenate([ka, kr, kr], 1), np.concatenate([ka, krs, krs], 1)]
        blocks.append(np.concatenate([w[:, o_va:o_va + 64], w[:, o_iw:o_iw + 4], w[:, 0:60]], 1))
        wx = np.concatenate(blocks, axis=1)
        assert wx.shape == (1024, NX_COLS), wx.shape
        wins.append(wx)
        sm = np.zeros((128, 16), np.float32)
        g = np.asarray(inp["ev_idx_k_g"][i], np.float32)
        b = np.asarray(inp["ev_idx_k_b"][i], np.float32)
        p = np.arange(128)
        sm[:, 0] = g[p % 64]
        sm[:, 1] = b[p % 64]
        sm[:, 2] = g[(p % 64 + 32) % 64]
        sm[:, 3] = b[(p % 64 + 32) % 64]
        sm[:, 4:7] = np.asarray(inp["ev_q_norm_g"][i], np.float32).reshape(3, 128).T
        sm[:, 7:9] = np.asarray(inp["ev_kv_norm_g"][i], np.float32).reshape(2, 128).T
        smalls.append(sm)
        uq = np.asarray(inp["ev_w_uq"][i], np.float32)
        uqs = uq.copy()
        for h in range(8):
            uqs[:, h * 96 + 64:h * 96 + 96] = _swap_half(uq, h * 96 + 64, 32)
        wuq.append(uq)
        wuqs.append(uqs)
        ukv = np.asarray(inp["ev_w_ukv"][i], np.float32).reshape(256, 8, 128)
        wkk.append(ukv[:, :, 0:64].reshape(256, 512))
        wkv.append(ukv[:, :, 64:128].reshape(256, 512))
    sh["ev_win"] = f(np.stack(wins))
    sh["ev_small"] = f(np.stack(smalls))
    sh["ev_wuq"] = f(np.stack(wuq))
    sh["ev_wuq_sw"] = f(np.stack(wuqs))
    sh["ev_wukv_k"] = f(np.stack(wkk))
    sh["ev_wukv_v"] = f(np.stack(wkv))
    sh["ev_wout"] = f(np.asarray(inp["ev_w_out"])[:ne])
    sh["od_wqkv"] = f(np.asarray(inp["od_w_qkv"])[:no])
    sh["od_wout"] = f(np.asarray(inp["od_w_out"])[:no])
    sh["ffn_wup"] = f(np.asarray(inp["ffn_w_up"])[:L])
    sh["ffn_cw"] = f(np.asarray(inp["ffn_conv_w"])[:L].transpose(0, 2, 1).reshape(L, NFC, 128, 3).transpose(0, 2, 1, 3))
    sh["ffn_cb"] = f(np.asarray(inp["ffn_conv_b"])[:L].reshape(L, NFC, 128).transpose(0, 2, 1))
    sh["ffn_wdown"] = f(np.asarray(inp["ffn_w_down"])[:L])
    cf, cb = host_consts()
    sh["cf32"] = cf
    sh["cbf"] = cb
    return sh


def core_inputs(inp, shared, seqs):
    m = dict(shared)
    m["x"] = np.ascontiguousarray(np.asarray(inp["x"], np.float32)[seqs])
    cT = np.asarray(inp["c"], np.float32)[seqs].T
    m["cT"] = np.ascontiguousarray(cT.reshape(8, 128, len(seqs)).transpose(1, 0, 2))
    m["pos"] = np.ascontiguousarray(np.asarray(inp["positions"], np.int32)[seqs])
    return m


_CACHE = {}


def run(inp, S, NSEQ, kinds, TOPK, ncores):
    key = (S, NSEQ, tuple(kinds), TOPK)
    if key not in _CACHE:
        _CACHE[key] = build(S, NSEQ, kinds, TOPK)[0]
    nc = _CACHE[key]
    shared = host_prep(inp, kinds)
    in_maps = [core_inputs(inp, shared, list(range(c * NSEQ, (c + 1) * NSEQ))) for c in range(ncores)]
    res = run_bass_kernel_spmd(nc, in_maps, core_ids=list(range(ncores)))
    return np.concatenate([np.asarray(r["out"]) for r in res.results], axis=0)


def kernel(**inputs):
    out = run(inputs, 2048, 2, ["even", "odd", "even", "odd"], 256, 8)
    return out.astype(np.float32)
```

```python
import math
from contextlib import ExitStack

import numpy as np
import concourse.bass as bass
import concourse.mybir as mybir
from concourse.bass_utils import run_bass_kernel_spmd

F32, BF16, I32 = mybir.dt.float32, mybir.dt.bfloat16, mybir.dt.int32
AF = mybir.ActivationFunctionType
ALU = mybir.AluOpType

D = 1024
DFF = 2816
NFC = DFF // 128
ALPHA = 8 ** 0.25
LN_EPS = 1e-5
RMS_EPS = 1e-6
NEG = -1.0e30
NEG2 = -3.0e30
MASKV = -30000.0
TWO_PI = 2.0 * math.pi


class Res:
    __slots__ = ("lw", "rd")

    def __init__(self):
        self.lw = None
        self.rd = {}


class Eng:
    def __init__(self, name, sid):
        self.name = name
        self.sid = sid
        self.cnt = 0
        self.seen = {}
        self.ops = []


class Kern:
    NDS = 8

    def __init__(self, nc, stack):
        self.nc = nc
        self.sems = []
        self.stack = stack
        self.E = {}
        for n in ("pe", "act", "dve", "pool", "sp"):
            self.E[n] = Eng(n, self._mk("s_" + n))
        self.dsem = {q: [self._mk(f"d_{q}{i}") for i in range(self.NDS)] for q in ("sp", "pool", "act")}
        self.dval = {}
        self.dnext = {"sp": 0, "pool": 0, "act": 0}
        self.nops = 0

    def _mk(self, name):
        h = self.stack.enter_context(self.nc.semaphore(name))
        self.sems.append(h)
        return len(self.sems) - 1

    def _deps(self, E, R, W):
        need = {}
        for r in R:
            if r.lw is not None:
                sid, v = r.lw
                if need.get(sid, 0) < v:
                    need[sid] = v
        for w in W:
            if w.lw is not None:
                sid, v = w.lw
                if sid != E.sid and need.get(sid, 0) < v:
                    need[sid] = v
            for sid, v in w.rd.items():
                if sid != E.sid and need.get(sid, 0) < v:
                    need[sid] = v
        for sid, v in need.items():
            if sid == E.sid and E.name == "pe":
                continue
            if E.seen.get(sid, 0) >= v:
                continue
            E.seen[sid] = v
            E.ops.append(("w", sid, v))
            self.nops += 1

    def op(self, en, fn, R=(), W=()):
        E = self.E[en]
        self._deps(E, R, W)
        E.cnt += 1
        E.ops.append(("o", fn))
        self.nops += 1
        for r in R:
            if r.rd.get(E.sid, 0) < E.cnt:
                r.rd[E.sid] = E.cnt
        tok = (E.sid, E.cnt)
        for w in W:
            w.lw = tok
            w.rd = {}

    def dma(self, q, out, in_, R=(), W=(), slow=False):
        E = self.E[q]
        pool = self.dsem[q]
        i = self.dnext[q]
        self.dnext[q] = (i + 1) % len(pool)
        sid = pool[i]
        prev = self.dval.get(sid, 0)
        if prev > 0 and E.seen.get(sid, 0) < prev:
            E.seen[sid] = prev
            E.ops.append(("w", sid, prev))
        self._deps(E, R, W)
        self.dval[sid] = prev + 16
        E.ops.append(("d", out, in_, sid, slow))
        self.nops += 1
        for r in R:
            r.rd[sid] = prev + 16
        tok = (sid, prev + 16)
        for w in W:
            w.lw = tok
            w.rd = {}

    def barrier(self):
        for E in self.E.values():
            for F in self.E.values():
                if F is E or F.cnt == 0:
                    continue
                if E.seen.get(F.sid, 0) < F.cnt:
                    E.seen[F.sid] = F.cnt
                    E.ops.append(("w", F.sid, F.cnt))
                    self.nops += 1
            for sid, v in self.dval.items():
                if E.seen.get(sid, 0) < v:
                    E.seen[sid] = v
                    E.ops.append(("w", sid, v))
                    self.nops += 1

    def flush(self, final=False):
        nc = self.nc
        if final:
            E = self.E["sp"]
            for sid, v in self.dval.items():
                if E.seen.get(sid, 0) < v:
                    E.seen[sid] = v
                    E.ops.append(("w", sid, v))
        sems = self.sems

        def run(E, eng):
            own = sems[E.sid]
            for o in E.ops:
                if o[0] == "w":
                    eng.wait_ge(sems[o[1]], o[2])
                elif o[0] == "o":
                    o[1](eng).then_inc(own, 1)
                else:
                    if o[4]:
                        eng.dma_start(out=o[1], in_=o[2], allow_slow_non_contiguous=True).then_inc(sems[o[3]], 16)
                    else:
                        eng.dma_start(out=o[1], in_=o[2]).then_inc(sems[o[3]], 16)
            E.ops = []

        with nc.Block() as blk:
            @blk.tensor
            def _(e):
                run(self.E["pe"], e)

            @blk.scalar
            def _(e):
                run(self.E["act"], e)

            @blk.vector
            def _(e):
                run(self.E["dve"], e)

            @blk.gpsimd
            def _(e):
                run(self.E["pool"], e)

            @blk.sync
            def _(e):
                run(self.E["sp"], e)


def mm(K, out, lhsT, rhs, start, stop, R, W):
    K.op("pe", lambda e: e.matmul(out, lhsT, rhs, start=start, stop=stop), R, W)


def tr(K, out, in_, ident, R, W):
    K.op("pe", lambda e: e.transpose(out, in_, ident), R, W)


def act(K, out, in_, func, R, W, bias=None, scale=None):
    kw = {}
    if bias is not None:
        kw["bias"] = bias
    if scale is not None:
        kw["scale"] = scale
    K.op("act", lambda e: e.activation(out, in_, func, **kw), R, W)


def ts(K, en, out, in0, s1, s2, op0, op1, R, W):
    if op1 is None:
        K.op(en, lambda e: e.tensor_scalar(out, in0, s1, None, op0), R, W)
    else:
        K.op(en, lambda e: e.tensor_scalar(out, in0, s1, s2, op0, op1), R, W)


def tt(K, en, out, in0, in1, op, R, W):
    K.op(en, lambda e: e.tensor_tensor(out, in0, in1, op), R, W)


def stt(K, out, in0, scalar, in1, op0, op1, R, W):
    K.op("dve", lambda e: e.scalar_tensor_tensor(out, in0, scalar, in1, op0, op1), R, W)


def cpy(K, en, out, in_, R, W):
    if en == "act":
        K.op("act", lambda e: e.activation(out, in_, AF.Copy), R, W)
    else:
        K.op(en, lambda e: e.tensor_copy(out, in_), R, W)


def recip(K, out, in_, R, W):
    K.op("dve", lambda e: e.reciprocal(out, in_), R, W)


def mset(K, en, ap, val, W):
    K.op(en, lambda e: e.memset(ap, val), (), W)


def build(S, NSEQ, kinds, TOPK):
    L = len(kinds)
    NE = max(1, sum(1 for k in kinds if k == "even"))
    NO = max(1, sum(1 for k in kinds if k == "odd"))
    NT = S // 128
    NR = S // 512
    nc = bass.Bass("TRN2", target_bir_lowering=False)

    def din(name, shape, dt=F32):
        return nc.dram_tensor(name, list(shape), dt, kind="ExternalInput").ap()

    x_d = din("x", [NSEQ, S, D])
    cT_d = din("cT", [128, 8, NSEQ])
    pos_d = din("pos", [NSEQ, S], I32)
    modw_d = din("mod_w", [L, D, 6 * D])
    modb_d = din("mod_bT", [L, 128, 48])
    lnp_d = din("lnp", [L, 4, D])
    win_d = din("ev_win", [NE, D, NX_COLS])
    evs_d = din("ev_small", [NE, 128, 16])
    wuq_d = din("ev_wuq", [NE, 384, 768])
    wuqs_d = din("ev_wuq_sw", [NE, 384, 768])
    wukk_d = din("ev_wukv_k", [NE, 256, 512])
    wukv_d = din("ev_wukv_v", [NE, 256, 512])
    ewo_d = din("ev_wout", [NE, D, D])
    oqkv_d = din("od_wqkv", [NO, D, 3 * D])
    owo_d = din("od_wout", [NO, D, D])
    wup_d = din("ffn_wup", [L, D, 2 * DFF])
    fcw_d = din("ffn_cw", [L, 128, NFC, 3])
    fcb_d = din("ffn_cb", [L, 128, NFC])
    wdn_d = din("ffn_wdown", [L, DFF, D])
    cf_d = din("cf32", [128, 3 * 128 + 8])
    cb_d = din("cbf", [128, 4 * 128 + 512])
    out_d = nc.dram_tensor("out", [NSEQ, S, D], F32, kind="ExternalOutput").ap()
    tabs_d = nc.dram_tensor("rope_tabs", [NSEQ, 4, 128, S], BF16).ap()

    with ExitStack() as st:
        K = Kern(nc, st)

        def sb(name, shape, dt):
            return st.enter_context(nc.sbuf_tensor(name, list(shape), dt))

        X = sb("X", [128, NT, D], F32)
        HT = sb("HT", [128, 8, S], BF16)
        CF = sb("CF", [128, 3 * 128 + 8], F32)
        CB = sb("CB", [128, 4 * 128 + 512], BF16)
        MODT = sb("MODT", [128, L, 48, NSEQ], F32)
        MODB = sb("MODB", [128, L, 48], F32)
        CACT = sb("CACT", [128, 8, NSEQ], F32)
        STAT = sb("STAT", [128, 32], F32)
        PH_F = 87 * 256
        PH = sb("PH", [128, PH_F], F32)
        ph_off = [0]
        VT2 = [None]

        def ph_reset(to=0):
            K.barrier()
            ph_off[0] = to
            VT2[0] = None

        def ph(nelem, dt):
            nf = nelem if dt == F32 else (nelem + 1) // 2
            o = ph_off[0]
            ph_off[0] = o + nf
            assert ph_off[0] <= PH_F, ("phase region overflow", ph_off[0], PH_F)
            a = PH[:, o:o + nf]
            return a if dt == F32 else a.bitcast(BF16)

        PS2 = [st.enter_context(nc.psum_tensor(f"ps{i}", [128, 1024], F32)) for i in range(4)]
        PSB = [PS2[k // 2][:, (k % 2) * 512:(k % 2 + 1) * 512] for k in range(8)]
        rP = [Res() for _ in range(8)]

        IDENT = CF[:, 0:128]
        ONESF = CF[:, 128:256]
        BLK64 = CF[:, 256:384]
        FREQA = CF[:, 384:385]
        SGNA = CF[:, 385:386]
        FREQB = CF[:, 386:387]
        SGNB = CF[:, 387:388]
        ONESB = CB[:, 0:128]
        NEGONES = CB[:, 128:256]
        NEGU = CB[:, 256:384]
        TRI = CB[:, 384:512]
        IREP = CB[:, 512:1024]

        rX = [Res() for _ in range(NT)]
        rTABD = [Res() for _ in range(NSEQ)]
        rHT = [Res() for _ in range(NT)]
        rC = Res()
        rMOD = Res()
        rSTAT = Res()

        K.dma("sp", CF[:], cf_d[:, :], (), (rC,))
        K.dma("pool", CB[:], cb_d[:, :], (), (rC,))
        K.dma("sp", CACT[:], cT_d[:, :, :], (), (rMOD,))
        K.dma("sp", MODB[:], modb_d.rearrange("l p c -> p l c"), (), (rMOD,), slow=True)
        act(K, CACT[:], CACT[:], AF.Silu, (rMOD,), (rMOD,))
        ph_reset()
        MW = [ph(2048, F32), ph(2048, F32)]
        rMW = [Res(), Res()]
        NG = 6 * D // 256
        for l in range(L):
            for g in range(NG):
                slot = (l * NG + g) % 2
                mwv = MW[slot].rearrange("p (k n) -> p k n", k=8)
                K.dma("sp" if g % 2 == 0 else "act", mwv,
                      modw_d[l][:, g * 256:(g + 1) * 256].rearrange("(k p) n -> p k n", p=128),
                      (), (rMW[slot],))
                for j in range(2):
                    ch = g * 2 + j
                    for k in range(8):
                        mm(K, PSB[0][:, ch * NSEQ:(ch + 1) * NSEQ], mwv[:, k, j * 128:(j + 1) * 128],
                           CACT[:, k, :], k == 0, k == 7, (rMW[slot], rMOD), (rP[0],))
            tt(K, "dve", MODT[:, l, :, :], PSB[0][:, 0:48 * NSEQ].rearrange("p (c s) -> p c s", s=NSEQ),
               MODB[:, l, :].unsqueeze(2).to_broadcast([128, 48, NSEQ]), ALU.add, (rP[0], rMOD), (rMOD,))
        for l in range(L):
            for lo in (8, 32):
                ts(K, "dve", MODT[:, l, lo:lo + 16, :], MODT[:, l, lo:lo + 16, :], 1.0, None, ALU.add, None,
                   (rMOD,), (rMOD,))
        K.flush()

        TRB = (6, 7)

        def transpose_mod(i, l, sub, s):
            base = 0 if sub == 0 else 24
            for half in range(2):
                b = TRB[half]
                for j in range(4):
                    ch = half * 4 + j
                    tr(K, PSB[b][:, j * 128:(j + 1) * 128], X[:, i, ch * 128:(ch + 1) * 128], IDENT,
                       (rX[i], rC), (rP[b],))
                for j in range(4):
                    ch = half * 4 + j
                    act(K, HT[:, ch, i * 128:(i + 1) * 128], PSB[b][:, j * 128:(j + 1) * 128], AF.Identity,
                        (rP[b], rMOD), (rHT[i],),
                        bias=MODT[:, l, base + ch, s:s + 1], scale=MODT[:, l, base + 8 + ch, s:s + 1])

        BCG = sb("BCG", [128, D], F32)
        BCLG = sb("BCLG", [128, D], F32)
        BCLB = sb("BCLB", [128, D], F32)
        VT = sb("VT", [128, 1, D], F32)
        DGT = sb("DGT", [128, 1, 512], F32)
        rBC = Res()
        rVT = [Res(), Res()]
        rDG = [Res(), Res()]

        def setup_epilogue(l, sub, s):
            gbase = 16 if sub == 0 else 40
            for half in range(2):
                b = TRB[half]
                for j in range(4):
                    ch = half * 4 + j
                    dg = DGT[:, 0, j * 128:(j + 1) * 128]
                    rr = rDG[0]
                    ts(K, "dve", dg, IDENT, MODT[:, l, gbase + ch, s:s + 1], None, ALU.mult, None, (rC, rMOD), (rr,))
                    mm(K, PSB[b][:, j * 128:(j + 1) * 128], ONESF, dg, True, True, (rr, rC), (rP[b],))
                cpy(K, "act", BCG[:, half * 512:(half + 1) * 512], PSB[b][:, :], (rP[b],), (rBC,))
            gi = 0 if sub == 0 else 2
            K.dma("sp", BCLG[:], lnp_d[l, gi:gi + 1, :].to_broadcast([128, D]), (), (rBC,))
            K.dma("sp", BCLB[:], lnp_d[l, gi + 1:gi + 2, :].to_broadcast([128, D]), (), (rBC,))

        def epilogue(i, ybanks, s, nxt, last, alpha=ALPHA):
            v = VT[:, 0, :] if (i % 2 == 0 or VT2[0] is None) else VT2[0]
            rv = rVT[0] if (i % 2 == 0 or VT2[0] is None) else rVT[1]
            for h in range(2):
                tt(K, "dve", v[:, h * 512:(h + 1) * 512], PSB[ybanks[h]][:, :], BCG[:, h * 512:(h + 1) * 512],
                   ALU.mult, (rP[ybanks[h]], rBC), (rv,))
            stt(K, v, X[:, i, :], alpha, v, ALU.mult, ALU.add, (rX[i], rv), (rv,))
            so = (i % 2) * 16
            for h in range(2):
                K.op("dve", lambda e, h=h: e.bn_stats(STAT[:, so + h * 6: so + h * 6 + 6], v[:, h * 512:(h + 1) * 512]),
                     (rv,), (rSTAT,))
            K.op("dve", lambda e: e.bn_aggr(STAT[:, so + 12: so + 14], STAT[:, so: so + 12]), (rSTAT,), (rSTAT,))
            act(K, STAT[:, so + 14: so + 15], STAT[:, so + 13: so + 14], AF.Sqrt, (rSTAT,), (rSTAT,), bias=EPS_LN[:, 0:1])
            recip(K, STAT[:, so + 14: so + 15], STAT[:, so + 14: so + 15], (rSTAT,), (rSTAT,))
            stt(K, STAT[:, so + 15: so + 16], STAT[:, so + 12: so + 13], -1.0, STAT[:, so + 14: so + 15], ALU.mult, ALU.mult,
                (rSTAT,), (rSTAT,))
            act(K, v, v, AF.Identity, (rv, rSTAT), (rv,), bias=STAT[:, so + 15: so + 16], scale=STAT[:, so + 14: so + 15])
            tt(K, "pool", v, v, BCLG[:], ALU.mult, (rv, rBC), (rv,))
            tt(K, "pool", X[:, i, :], v, BCLB[:], ALU.add, (rv, rBC), (rX[i],))

        def epilogue_b(i, s, nxt, last):
            if last:
                K.dma("sp", out_d[s, i * 128:(i + 1) * 128, :], X[:, i, :], (rX[i],), ())
            else:
                transpose_mod(i, nxt[0], nxt[1], s)

        EPS_LN = sb("EPSC", [128, 4], F32)
        rEPS = Res()
        mset(K, "dve", EPS_LN[:, 0:1], LN_EPS, (rSTAT,))
        mset(K, "dve", EPS_LN[:, 1:2], RMS_EPS, (rSTAT,))
        mset(K, "dve", EPS_LN[:, 2:3], 1.0, (rSTAT,))
        mset(K, "dve", EPS_LN[:, 3:4], 0.0, (rSTAT,))

        def ffn(l, s, nxt, last):
            ph_reset()
            HC = NFC // 2
            UT = ph(HC * S, BF16).rearrange("p (c t) -> p c t", c=HC)
            WD = ph(HC * D, BF16).rearrange("p (c n) -> p c n", c=HC)
            WU = [ph(2048, BF16).rearrange("p (k n) -> p k n", k=8) for k in range(2)]
            rUT = [Res() for _ in range(HC)]
            rWD = Res()
            rWUa = [Res(), Res()]
            rWUg = [Res(), Res()]
            AB = [ph(516, F32), ph(516, F32)]
            TT_ = [ph(512, F32), ph(512, F32)]
            rAB = [Res(), Res()]
            rTT = [Res(), Res()]
            CW = ph(NFC * 3, F32).rearrange("p (c t) -> p c t", t=3)
            CBI = ph(NFC, F32)
            rCW = Res()
            VT2[0] = ph(D, F32)
            rVT[1] = Res()
            K.dma("sp", CW, fcw_d[l], (), (rCW,))
            K.dma("sp", CBI, fcb_d[l], (), (rCW,))
            setup_epilogue(l, 1, s)
            it = 0
            it2 = 0
            for half in range(2):
                for cl in range(HC):
                    c = half * HC + cl
                    if cl == 2:
                        K.dma("pool", WD, wdn_d[l][half * HC * 128:(half + 1) * HC * 128, :].rearrange("(c p) n -> p c n", p=128),
                              (), (rWD,))
                    slot = it % 2
                    it += 1
                    wu = WU[slot]
                    K.dma("pool", wu[:, :, 0:128], wup_d[l][:, c * 128:(c + 1) * 128].rearrange("(k p) n -> p k n", p=128),
                          (), (rWUa[slot],))
                    K.dma("pool", wu[:, :, 128:256],
                          wup_d[l][:, DFF + c * 128:DFF + (c + 1) * 128].rearrange("(k p) n -> p k n", p=128),
                          (), (rWUg[slot],))
                    for r in range(NR):
                        t0 = r * 512
                        s2 = it2 % 2
                        s3 = it2 % 3
                        it2 += 1
                        ba, bg = s3, 3 + s3
                        hts = [rHT[r * 4 + j] for j in range(4)]
                        for k in range(8):
                            mm(K, PSB[ba][:, :], wu[:, k, 0:128], HT[:, k, t0:t0 + 512], k == 0, k == 7,
                               [rWUa[slot]] + hts, (rP[ba],))
                        for k in range(8):
                            mm(K, PSB[bg][:, :], wu[:, k, 128:256], HT[:, k, t0:t0 + 512], k == 0, k == 7,
                               [rWUg[slot]] + hts, (rP[bg],))
                        ab, rab = AB[s2], rAB[s2]
                        if r == 0:
                            mset(K, "pool", ab[:, 0:2], 0.0, (rab,))
                        else:
                            cpy(K, "pool", ab[:, 0:2], AB[1 - s2][:, 512:514], (rAB[1 - s2],), (rab,))
                        cpy(K, "act", ab[:, 2:514], PSB[ba][:, :], (rP[ba],), (rab,))
                        T = TT_[s2]
                        rT = rTT[s2]
                        ts(K, "dve", T, ab[:, 2:514], CW[:, c, 2:3], CBI[:, c:c + 1], ALU.mult, ALU.add, (rab, rCW), (rT,))
                        stt(K, T, ab[:, 1:513], CW[:, c, 1:2], T, ALU.mult, ALU.add, (rab, rCW, rT), (rT,))
                        stt(K, T, ab[:, 0:512], CW[:, c, 0:1], T, ALU.mult, ALU.add, (rab, rCW, rT), (rT,))
                        act(K, T, T, AF.Silu, (rT,), (rT,))
                        tt(K, "dve", UT[:, cl, t0:t0 + 512], T, PSB[bg][:, :], ALU.mult, (rT, rP[bg]), (rUT[cl],))
                YB = [(4, 5), (6, 7)] if half == 0 else [(4, 5), (2, 3)]
                for i2 in range(NT + 2):
                    if i2 < NT:
                        i = i2
                        yb = YB[i % 2]
                        for h in range(2):
                            b = yb[h]
                            for cl in range(HC):
                                mm(K, PSB[b][:, :], UT[:, cl, i * 128:(i + 1) * 128], WD[:, cl, h * 512:(h + 1) * 512],
                                   cl == 0, cl == HC - 1, (rUT[cl], rWD), (rP[b],))
                    if 1 <= i2 <= NT:
                        i = i2 - 1
                        yb = YB[i % 2]
                        if half == 0:
                            v = VT[:, 0, :] if i % 2 == 0 else VT2[0]
                            rv = rVT[i % 2]
                            for h in range(2):
                                tt(K, "dve", v[:, h * 512:(h + 1) * 512], PSB[yb[h]][:, :], BCG[:, h * 512:(h + 1) * 512],
                                   ALU.mult, (rP[yb[h]], rBC), (rv,))
                            stt(K, X[:, i, :], X[:, i, :], ALPHA, v, ALU.mult, ALU.add, (rX[i], rv), (rX[i],))
                        else:
                            epilogue(i, yb, s, nxt, last, alpha=1.0)
                    if i2 >= 2 and half == 1:
                        epilogue_b(i2 - 2, s, nxt, last)

        def run_pipeline(units, nstage, on=True):
            n = len(units)
            if not on:
                for u in units:
                    for st_ in u:
                        st_()
                return
            for t in range(n + nstage - 1):
                for sidx in range(nstage):
                    u = t - sidx
                    if 0 <= u < n:
                        units[u][sidx]()

        def _sb_unit(g, hp, r, hh, b_, sl, first, evac, QT, KT, V, YT, E32, SPB, ATB, RB, rQ, rK, rV, rR, rE, rSP, rAT, rYT):
            NSL = len(E32)
            ub = sl
            q0 = r * 512
            p0 = hh * 64
            kb0 = b_ * 128
            c0 = max(0, kb0 - q0)
            ncol = 512 - c0
            diag = kb0 >= q0
            P0, P1 = ub % 2, 2 + ub % 2
            lhsK = KT[p0:p0 + 64, hp, kb0:kb0 + 128]
            rhsQ = QT[p0:p0 + 64, hp, q0 + c0:q0 + 512]

            def stA():
                mm(K, PSB[P0][:, 0:ncol], lhsK, rhsQ, True, True, (rK, rQ), (rP[P0],))
                act(K, E32[sl][:, 0:ncol], PSB[P0][:, 0:ncol], AF.Exp, (rP[P0],), (rE[sl],))
                act(K, SPB[sl][:, 0:ncol], E32[sl][:, 0:ncol], AF.Ln, (rE[sl], rSTAT), (rSP[sl],), bias=EPS_LN[:, 2:3])
                if diag:
                    tt(K, "pool", SPB[sl][:, 0:128], SPB[sl][:, 0:128], TRI, ALU.mult, (rSP[sl], rC), (rSP[sl],))

            def stB():
                if first:
                    mset(K, "pool", RB, 0.0, (rR,))
                mm(K, PSB[P1][:, 0:ncol], NEGU, SPB[sl][:, 0:ncol], True, first, (rC, rSP[sl]), (rP[P1],))
                if not first:
                    mm(K, PSB[P1][:, 0:ncol], NEGONES, RB[:, c0:512], False, True, (rC, rR), (rP[P1],))
                act(K, ATB[sl][:, 0:ncol], PSB[P1][:, 0:ncol], AF.Exp, (rP[P1],), (rAT[sl],))
                tt(K, "dve", ATB[sl][:, 0:ncol], E32[sl][:, 0:ncol], ATB[sl][:, 0:ncol], ALU.mult, (rE[sl], rAT[sl]), (rAT[sl],))
                if diag:
                    tt(K, "pool", ATB[sl][:, 0:128], ATB[sl][:, 0:128], TRI, ALU.mult, (rAT[sl], rC), (rAT[sl],))
                if b_ > 0:
                    tt(K, "dve", RB[:, c0:512], RB[:, c0:512], SPB[sl][:, 0:ncol], ALU.add, (rR, rSP[sl]), (rR,))

            def stC():
                mm(K, PSB[6][p0:p0 + 64, c0:512], V[:, b_, (hp * 2 + hh) * 64:(hp * 2 + hh + 1) * 64],
                   ATB[sl][:, 0:ncol], first, b_ == 0, (rV, rAT[sl]), (rP[6],))
                if evac:
                    cpy(K, "act", YT[:, g * 2 + hp, q0:q0 + 512], PSB[6][:, :], (rP[6],), (rYT[r],))

            return (stA, (lambda: None), stB, stC)

        def _mla_unit(hp, r, hh, b_, nb, u, QB, KB, VB, PT, RC, YT, rQB, rKB, rVB, rPT, rRC, sc_mla):
            q0 = r * 512
            p0 = hh * 64
            kb0 = b_ * 128
            c0 = max(0, kb0 - q0)
            ncol = 512 - c0
            z = 4 + u % 2
            sl = u % len(PT)

            def stA():
                mm(K, PSB[z][:, 0:ncol], KB[0:96, hh, kb0:kb0 + 128], QB[0:96, hh, q0 + c0:q0 + 512], True, True,
                   (rKB, rQB), (rP[z],))
                act(K, PT[sl][:, 0:ncol], PSB[z][:, 0:ncol], AF.Exp, (rP[z],), (rPT[sl],), scale=sc_mla)
                if kb0 >= q0:
                    mset(K, "pool", PT[sl][64:128, 0:64], 0.0, (rPT[sl],))

            def stB():
                mm(K, PSB[6][p0:p0 + 64, c0:512], VB[:, b_, hh * 64:(hh + 1) * 64], PT[sl][:, 0:ncol],
                   b_ == 0, b_ == nb - 1, (rVB, rPT[sl]), (rP[6],))
                mm(K, PSB[7][p0:p0 + 64, c0:512], ONESB[:, 0:64], PT[sl][:, 0:ncol],
                   b_ == 0, b_ == nb - 1, (rC, rPT[sl]), (rP[7],))
                if hh == 1 and b_ == nb - 1:
                    recip(K, RC, PSB[7][:, :], (rP[7],), (rRC,))
                    tt(K, "dve", YT[:, 4 + hp, q0:q0 + 512], PSB[6][:, :], RC, ALU.mult, (rP[6], rRC),
                       [rHT[r * 4 + q] for q in range(4)])

            return (stA, stB)

        def _sb_unit2(g, hp, r, b_, u, first, evac, QT, KT, V, YT, E32, SPB, ATB, RB, rQ, rK, rV, rR, rE, rSP, rAT, rYT):
            sl = u % 2
            q0 = r * 512
            kb0 = b_ * 128
            c0 = max(0, kb0 - q0)
            ncol = 512 - c0
            diag = kb0 >= q0
            pa = u % 2
            P0v = PS2[pa].rearrange("p (h t) -> p h t", h=2)
            P1v = PS2[2].rearrange("p (h t) -> p h t", h=2)
            rP0 = (rP[2 * pa], rP[2 * pa + 1])
            rP1 = (rP[4], rP[5])
            TRIB = TRI.unsqueeze(1).to_broadcast([128, 2, 128])

            def stA():
                for hh in range(2):
                    p0 = hh * 64
                    mm(K, P0v[:, hh, 0:ncol], KT[p0:p0 + 64, hp, kb0:kb0 + 128], QT[p0:p0 + 64, hp, q0 + c0:q0 + 512],
                       True, True, (rK, rQ), (rP0[hh],))
                act(K, E32[sl][:, :, 0:ncol], P0v[:, :, 0:ncol], AF.Exp, rP0, (rE[sl],))
                act(K, SPB[sl][:, :, 0:ncol], E32[sl][:, :, 0:ncol], AF.Ln, (rE[sl],), (rSP[sl],), bias=1.0)
                if diag:
                    tt(K, "pool", SPB[sl][:, :, 0:128], SPB[sl][:, :, 0:128], TRIB, ALU.mult, (rSP[sl], rC), (rSP[sl],))

            def stB():
                if first:
                    mset(K, "pool", RB, 0.0, (rR,))
                for hh in range(2):
                    mm(K, P1v[:, hh, 0:ncol], NEGU, SPB[sl][:, hh, 0:ncol], True, first, (rC, rSP[sl]), (rP1[hh],))
                    if not first:
                        mm(K, P1v[:, hh, 0:ncol], NEGONES, RB[:, hh, c0:512], False, True, (rC, rR), (rP1[hh],))
                act(K, ATB[sl][:, :, 0:ncol], P1v[:, :, 0:ncol], AF.Exp, rP1, (rAT[sl],))
                tt(K, "dve", ATB[sl][:, :, 0:ncol], E32[sl][:, :, 0:ncol], ATB[sl][:, :, 0:ncol], ALU.mult,
                   (rE[sl], rAT[sl]), (rAT[sl],))
                if diag:
                    tt(K, "pool", ATB[sl][:, :, 0:128], ATB[sl][:, :, 0:128], TRIB, ALU.mult, (rAT[sl], rC), (rAT[sl],))
                if b_ > 0:
                    tt(K, "dve", RB[:, :, c0:512], RB[:, :, c0:512], SPB[sl][:, :, 0:ncol], ALU.add, (rR, rSP[sl]), (rR,))

            def stC():
                for hh in range(2):
                    p0 = hh * 64
                    mm(K, PSB[6][p0:p0 + 64, c0:512], V[:, b_, (hp * 2 + hh) * 64:(hp * 2 + hh + 1) * 64],
                       ATB[sl][:, hh, 0:ncol], first, b_ == 0, (rV, rAT[sl]), (rP[6],))
                if evac:
                    cpy(K, "act", YT[:, g * 2 + hp, q0:q0 + 512], PSB[6][:, :], (rP[6],), (rYT[r],))

            return (stA, stB, stC)

        def odd_mixer(l, s):
            io = sum(1 for k in kinds[:l] if k == "odd")
            ph_reset()
            YT = ph(8 * S, BF16).rearrange("p (c t) -> p c t", c=8)
            rYT = [Res() for _ in range(NR)]
            mark = ph_off[0]
            W3 = [ph(2048, BF16).rearrange("p (k n) -> p k n", k=8) for _ in range(3)]
            rW3 = [Res(), Res(), Res()]
            QT = ph(2 * S, BF16).rearrange("p (c t) -> p c t", c=2)
            KT = ph(2 * S, BF16).rearrange("p (c t) -> p c t", c=2)
            V = ph(NT * 256, BF16).rearrange("p (i n) -> p i n", i=NT)
            rQ, rK, rV, rR = Res(), Res(), Res(), Res()
            NSL = 2
            E32 = [ph(1024, F32).rearrange("p (h t) -> p h t", h=2) for _ in range(NSL)]
            SPB = [ph(1024, BF16).rearrange("p (h t) -> p h t", h=2) for _ in range(NSL)]
            ATB = [ph(1024, BF16).rearrange("p (h t) -> p h t", h=2) for _ in range(NSL)]
            RB = ph(1024, BF16).rearrange("p (h t) -> p h t", h=2)
            rE = [Res() for _ in range(NSL)]
            rSP = [Res() for _ in range(NSL)]
            rAT = [Res() for _ in range(NSL)]
            setup_epilogue(l, 0, s)
            pb = 0
            unit = 0
            def load_w3(g):
                for j in range(3):
                    K.dma("pool", W3[j], oqkv_d[io][:, j * D + g * 256: j * D + (g + 1) * 256].rearrange("(k p) n -> p k n", p=128),
                          (), (rW3[j],))

            load_w3(0)
            for g in range(4):
                for dst, rdst, wi, scale in ((QT, rQ, 0, 0.125), (KT, rK, 1, 1.0)):
                    for j in range(2):
                        for r in range(NR):
                            b = pb % 2
                            pb += 1
                            hts = [rHT[r * 4 + q] for q in range(4)]
                            for k in range(8):
                                mm(K, PSB[b][:, :], W3[wi][:, k, j * 128:(j + 1) * 128], HT[:, k, r * 512:(r + 1) * 512],
                                   k == 0, k == 7, [rW3[wi]] + hts, (rP[b],))
                            act(K, dst[:, j, r * 512:(r + 1) * 512], PSB[b][:, :], AF.Copy, (rP[b],), (rdst,), scale=scale)
                for i in range(NT):
                    b = pb % 2
                    pb += 1
                    for k in range(8):
                        mm(K, PSB[b][:, 0:256], HT[:, k, i * 128:(i + 1) * 128], W3[2][:, k, :], k == 0, k == 7,
                           (rW3[2], rHT[i]), (rP[b],))
                    cpy(K, "dve", V[:, i, :], PSB[b][:, 0:256], (rP[b],), (rV,))
                units = []
                for hp in range(2):
                    for r in range(NR):
                        nb = 4 * (r + 1)
                        first = True
                        for b_ in reversed(range(nb)):
                            units.append(_sb_unit2(g, hp, r, b_, unit, first, b_ == 0,
                                                   QT, KT, V, YT, E32, SPB, ATB, RB, rQ, rK, rV, rR, rE, rSP, rAT, rYT))
                            unit += 1
                            first = False
                if g + 1 < 4:
                    load_w3(g + 1)
                run_pipeline(units, 3, PIPE_ODD)
            ph_reset(mark)
            WO = ph(8 * D, BF16).rearrange("p (k n) -> p k n", k=8)
            rWO = Res()
            VT2[0] = ph(D, F32)
            rVT[1] = Res()
            rWOh = [Res(), Res()]
            for h in range(2):
                K.dma("pool", WO[:, :, h * 512:(h + 1) * 512],
                      owo_d[io][:, h * 512:(h + 1) * 512].rearrange("(k p) n -> p k n", p=128), (), (rWOh[h],))
            YB = [(4, 5), (2, 3)]
            for i in range(NT + 2):
                if i < NT:
                    yb = YB[i % 2]
                    for h in range(2):
                        for k in range(8):
                            mm(K, PSB[yb[h]][:, :], YT[:, k, i * 128:(i + 1) * 128], WO[:, k, h * 512:(h + 1) * 512],
                               k == 0, k == 7, (rYT[i // 4], rWOh[h]), (rP[yb[h]],))
                if 1 <= i <= NT:
                    epilogue(i - 1, YB[(i - 1) % 2], s, (l, 1), False)
                if i >= 2:
                    epilogue_b(i - 2, s, (l, 1), False)


        def make_tables(s, dcos, dsin, FREQ, SGN, rT, Ta, Tb, Tc, Ti, rTmp):
            NC = Ta.shape[1]
            for c0 in range(0, S, NC):
                K.dma("sp", Ti, pos_d[s, c0:c0 + NC].partition_broadcast(128), (), (rTmp,))
                cpy(K, "dve", Ta, Ti, (rTmp,), (rTmp,))
                ts(K, "dve", Ta, Ta, FREQ, None, ALU.mult, None, (rTmp, rC), (rTmp,))
                for dst, phase in ((dsin, 0.0), (dcos, 0.25)):
                    ts(K, "dve", Tb, Ta, 1.0 / TWO_PI, phase, ALU.mult, ALU.add, (rTmp,), (rTmp,))
                    cpy(K, "dve", Ti, Tb, (rTmp,), (rTmp,))
                    cpy(K, "dve", Tb, Ti, (rTmp,), (rTmp,))
                    ts(K, "dve", Tb, Tb, -TWO_PI, phase * TWO_PI, ALU.mult, ALU.add, (rTmp,), (rTmp,))
                    tt(K, "dve", Tb, Tb, Ta, ALU.add, (rTmp,), (rTmp,))
                    ts(K, "dve", Tc, Tb, math.pi, TWO_PI, ALU.is_gt, ALU.mult, (rTmp,), (rTmp,))
                    tt(K, "dve", Tb, Tb, Tc, ALU.subtract, (rTmp,), (rTmp,))
                    ts(K, "dve", Tc, Tb, -math.pi, TWO_PI, ALU.is_lt, ALU.mult, (rTmp,), (rTmp,))
                    tt(K, "dve", Tb, Tb, Tc, ALU.add, (rTmp,), (rTmp,))
                    ts(K, "dve", Tb, Tb, math.pi, -math.pi, ALU.min, ALU.max, (rTmp,), (rTmp,))
                    act(K, Tb, Tb, AF.Sin, (rTmp,), (rTmp,))
                    if phase == 0.0:
                        ts(K, "dve", dst[:, c0:c0 + NC], Tb, SGN, None, ALU.mult, None, (rTmp, rC), (rT,))
                    else:
                        cpy(K, "dve", dst[:, c0:c0 + NC], Tb, (rTmp,), (rT,))

        def even_mixer(l, s):
            ie = sum(1 for k in kinds[:l] if k == "even")
            ph_reset()
            YT = HT
            EVS = ph(16, F32)
            rEVS = Res()
            K.dma("sp", EVS, evs_d[ie], (), (rEVS,))
            CQN = ph(3 * S, BF16).rearrange("p (c t) -> p c t", c=3)
            CKVN = ph(2 * S, BF16).rearrange("p (c t) -> p c t", c=2)
            KRR = ph(S, BF16)
            rLAT = Res()
            mark_lat = ph_off[0]
            QA = ph(4 * S, BF16).rearrange("p (c t) -> p c t", c=4)
            KA2 = ph(S, BF16)
            IQ = ph(2 * S, BF16).rearrange("p (c t) -> p c t", c=2)
            IK2 = ph(S, BF16)
            VA = ph(NT * 64, BF16).rearrange("p (i n) -> p i n", i=NT)
            IW = ph(NT * 4, F32).rearrange("p (i n) -> p i n", i=NT)
            rQA, rKA, rIQ, rIK, rVA = Res(), Res(), Res(), Res(), Res()
            mark_dsa = ph_off[0]
            TC_ = ph(S, BF16)
            TS_ = ph(S, BF16)
            rTAB = Res()
            rTAB2 = Res()
            WP = [ph(1024, BF16).rearrange("p (k n) -> p k n", k=8) for _ in range(3)]
            rWP = [Res(), Res(), Res()]
            SCR = ph(2048, F32)
            SS = [SCR[:, k * 512:(k + 1) * 512] for k in range(4)]
            rS = [Res() for _ in range(4)]
            SQB = ph(3 * 256, BF16).rearrange("p (c t) -> p c t", c=3)
            rSQ = Res()
            setup_epilogue(l, 0, s)
            K.dma("sp", TC_, tabs_d[s, 0], (rTABD[s],), (rTAB,))
            K.dma("act", TS_, tabs_d[s, 1], (rTABD[s],), (rTAB2,))
            pbk = [0]

            def proj(blk, slot, r0, n, bank):
                hts = [rHT[q] for q in range(r0 // 128, (r0 + n + 127) // 128)]
                for k in range(8):
                    mm(K, PSB[bank][:, 0:n], WP[slot][:, k, :], HT[:, k, r0:r0 + n], k == 0, k == 7,
                       [rWP[slot]] + hts, (rP[bank],))

            def loadw(blk, slot):
                K.dma("pool", WP[slot], win_d[ie][:, blk * 128:(blk + 1) * 128].rearrange("(k p) n -> p k n", p=128),
                      (), (rWP[slot],))

            def rope_out(dst, rdst, bA, bB, r0, p_lo, p_hi, TCt, TSt):
                t1, t2 = SS[0], SS[1]
                tt(K, "dve", t1[p_lo:p_hi, :], PSB[bA][p_lo:p_hi, :], TCt[p_lo:p_hi, r0:r0 + 512], ALU.mult,
                   (rP[bA], rTAB), (rS[0],))
                tt(K, "dve", t2[p_lo:p_hi, :], PSB[bB][p_lo:p_hi, :], TSt[p_lo:p_hi, r0:r0 + 512], ALU.mult,
                   (rP[bB], rTAB2), (rS[1],))
                tt(K, "pool", dst, t1[p_lo:p_hi, :], t2[p_lo:p_hi, :], ALU.add, (rS[0], rS[1]), (rdst,))

            ropes = [(j, 4 + j, (lambda r0, j=j: QA[:, j, r0:r0 + 512]), rQA) for j in range(4)]
            ropes.append((8, 9, (lambda r0: KA2[:, r0:r0 + 512]), rKA))
            ropes += [(10 + j, 12 + j, (lambda r0, j=j: IQ[:, j, r0:r0 + 512]), rIQ) for j in range(2)]
            sa, sb = 0, 1
            loadw(ropes[0][0], sa)
            loadw(ropes[0][1], sb)
            for pi_, (bA_, bB_, dstf, rd) in enumerate(ropes):
                free = 3 - sa - sb
                if pi_ + 1 < len(ropes):
                    loadw(ropes[pi_ + 1][0], free)
                for r in range(NR):
                    b0 = (pbk[0] % 2) * 2
                    pbk[0] += 1
                    proj(bA_, sa, r * 512, 512, b0)
                    proj(bB_, sb, r * 512, 512, b0 + 1)
                    rope_out(dstf(r * 512), rd, b0, b0 + 1, r * 512, 0, 128, TC_, TS_)
                if pi_ + 1 < len(ropes):
                    loadw(ropes[pi_ + 1][1], sa)
                    sa, sb = free, sa
            loadw(14, 0)
            loadw(15, 1)
            for r in range(NR):
                r0 = r * 512
                b0 = (pbk[0] % 2) * 2
                pbk[0] += 1
                proj(14, 0, r0, 512, b0)
                proj(15, 1, r0, 512, b0 + 1)
                cpy(K, "act", SS[0], PSB[b0][:, :], (rP[b0],), (rS[0],))
                cpy(K, "act", SS[1], PSB[b0 + 1][:, :], (rP[b0 + 1],), (rS[1],))
                mm(K, PSB[b0][:, :], BLK64, SS[0], True, True, (rC, rS[0]), (rP[b0],))
                tt(K, "dve", SS[0], SS[0], PSB[b0][:, :], ALU.subtract, (rS[0], rP[b0]), (rS[0],))
                tt(K, "dve", SS[1], SS[1], PSB[b0][:, :], ALU.subtract, (rS[1], rP[b0]), (rS[1],))
                act(K, SS[2], SS[0], AF.Square, (rS[0],), (rS[2],))
                mm(K, PSB[b0 + 1][:, :], BLK64, SS[2], True, True, (rC, rS[2]), (rP[b0 + 1],))
                act(K, SS[2], PSB[b0 + 1][:, :], AF.Sqrt, (rP[b0 + 1], rSTAT), (rS[2],), bias=EPS_LN[:, 0:1])
                recip(K, SS[2], SS[2], (rS[2],), (rS[2],))
                tt(K, "dve", SS[0], SS[0], SS[2], ALU.mult, (rS[0], rS[2]), (rS[0],))
                tt(K, "dve", SS[1], SS[1], SS[2], ALU.mult, (rS[1], rS[2]), (rS[1],))
                ts(K, "dve", SS[0], SS[0], EVS[:, 0:1], EVS[:, 1:2], ALU.mult, ALU.add, (rS[0], rEVS), (rS[0],))
                ts(K, "dve", SS[1], SS[1], EVS[:, 2:3], EVS[:, 3:4], ALU.mult, ALU.add, (rS[1], rEVS), (rS[1],))
                tt(K, "dve", SS[0], SS[0], TC_[:, r0:r0 + 512], ALU.mult, (rS[0], rTAB), (rS[0],))
                tt(K, "dve", SS[1], SS[1], TS_[:, r0:r0 + 512], ALU.mult, (rS[1], rTAB2), (rS[1],))
                tt(K, "pool", IK2[:, r0:r0 + 512], SS[0], SS[1], ALU.add, (rS[0], rS[1]), (rIK,))
            for blk0, nch, dstL, gcol, dim in ((16, 3, CQN, 4, 384.0), (19, 2, CKVN, 7, 256.0)):
                TMP = SCR[:, 0:nch * 256].rearrange("p (c t) -> p c t", c=nch)
                for j in range(nch):
                    loadw(blk0 + j, j)
                for rr in range(S // 256):
                    r0 = rr * 256
                    for j in range(nch):
                        b0 = pbk[0] % 2
                        pbk[0] += 1
                        proj(blk0 + j, j, r0, 256, b0)
                        cpy(K, "act", TMP[:, j, :], PSB[b0][:, 0:256], (rP[b0],), (rS[0], rS[1]))
                        act(K, SQB[:, j, :], PSB[b0][:, 0:256], AF.Square, (rP[b0],), (rSQ,))
                    for j in range(nch):
                        mm(K, PSB[2][:, 0:256], ONESB, SQB[:, j, :], j == 0, j == nch - 1, (rC, rSQ), (rP[2],))
                    act(K, SS[3][:, 0:256], PSB[2][:, 0:256], AF.Sqrt, (rP[2], rSTAT), (rS[3],), bias=EPS_LN[:, 1:2],
                        scale=1.0 / dim)
                    recip(K, SS[3][:, 0:256], SS[3][:, 0:256], (rS[3],), (rS[3],))
                    for j in range(nch):
                        stt(K, dstL[:, j, r0:r0 + 256], TMP[:, j, :], EVS[:, gcol + j:gcol + j + 1], SS[3][:, 0:256],
                            ALU.mult, ALU.mult, (rS[0], rS[1], rEVS, rS[3]), (rLAT,))
            loadw(23, 0)
            for i in range(NT):
                b0 = pbk[0] % 2
                pbk[0] += 1
                for k in range(8):
                    mm(K, PSB[b0][:, 0:128], HT[:, k, i * 128:(i + 1) * 128], WP[0][:, k, :], k == 0, k == 7,
                       (rWP[0], rHT[i]), (rP[b0],))
                cpy(K, "act", VA[:, i, :], PSB[b0][:, 0:64], (rP[b0],), (rVA,))
                ts(K, "dve", IW[:, i, :], PSB[b0][:, 64:68], 1.0 / 16.0, None, ALU.mult, None, (rP[b0],), (rVA,))
            K.dma("sp", TC_, tabs_d[s, 2], (rTABD[s],), (rTAB,))
            K.dma("act", TS_, tabs_d[s, 3], (rTABD[s],), (rTAB2,))
            loadw(21, 0)
            loadw(22, 1)
            for r in range(NR):
                b0 = (pbk[0] % 2) * 2
                pbk[0] += 1
                proj(21, 0, r * 512, 512, b0)
                proj(22, 1, r * 512, 512, b0 + 1)
                rope_out(KRR[64:96, r * 512:(r + 1) * 512], rLAT, b0, b0 + 1, r * 512, 64, 96, TC_, TS_)

            if DBG_MODE == 3:
                K.barrier()
                ov = out_d[s].rearrange("s d -> (s d)").rearrange("(c p t) -> p c t", c=8, p=128)
                dl = [QA[:, 0, 0:512], KA2[:, 0:512], IQ[:, 0, 0:512], IK2[:, 0:512], CQN[:, 0, 0:512], KRR[:, 0:512]]
                for c, a in enumerate(dl):
                    cpy(K, "dve", VT[:, 0, 0:512], a, (rHT[0],), (rVT[0],))
                    K.dma("sp", ov[:, c, 0:512], VT[:, 0, 0:512], (rVT[0],), ())
                mset(K, "dve", VT[:, 0, 0:512], 0.0, (rVT[0],))
                cpy(K, "dve", VT[:, 0, 0:256], VA[:, 0:4, :], (rHT[0],), (rVT[0],))
                cpy(K, "dve", VT[:, 0, 256:272], IW[:, 0:4, :], (rHT[0],), (rVT[0],))
                K.dma("sp", ov[:, 6, 0:512], VT[:, 0, 0:512], (rVT[0],), ())
                return
            ph_reset(mark_dsa)
            pairs = [(a, NT - 1 - a) for a in range(NT // 2)]
            PW = (NT + 1) * 128
            SCP = ph(PW, F32)
            MBP = [ph(PW, BF16), ph(PW, BF16)]
            RLS = [ph(512, F32) for _ in range(3)]
            PT = [ph(512, BF16) for _ in range(3)]
            RC = ph(512, F32)
            M8 = [ph(8, F32), ph(8, F32)]
            rSCt = [Res(), Res()]
            rMBt = [[Res(), Res()], [Res(), Res()]]
            rRLS = [Res() for _ in range(3)]
            rRC = Res()
            rM8 = [Res(), Res()]
            rPT = [Res() for _ in range(3)]
            cnt = {"ib": 0, "un": 0}

            def tile_view(buf, pr, w):
                a, b = pr
                off = 0 if w == 0 else (a + 1) * 128
                i = pr[w]
                return buf[:, off:off + (i + 1) * 128]

            def indexer(pr):
                for w in range(2):
                    i = pr[w]
                    t0 = i * 128
                    SK = (i + 1) * 128
                    SC = tile_view(SCP, pr, w)
                    for hh in range(4):
                        jh, p0 = hh // 2, (hh % 2) * 64
                        for k0 in range(0, SK, 512):
                            kn = min(512, SK - k0)
                            b = cnt["ib"] % 2
                            RL = RLS[cnt["ib"] % 3]
                            rRL = rRLS[cnt["ib"] % 3]
                            cnt["ib"] += 1
                            mm(K, PSB[b][:, 0:kn], IQ[p0:p0 + 64, jh, t0:t0 + 128], IK2[p0:p0 + 64, k0:k0 + kn], True, True,
                               (rIQ, rIK), (rP[b],))
                            if hh == 0:
                                act(K, RL[:, 0:kn], PSB[b][:, 0:kn], AF.Relu, (rP[b],), (rRL,))
                                ts(K, "dve", SC[:, k0:k0 + kn], RL[:, 0:kn], IW[:, i, 0:1], None, ALU.mult, None,
                                   (rRL, rVA), (rSCt[w],))
                            else:
                                act(K, RL[:, 0:kn], PSB[b][:, 0:kn], AF.Relu, (rP[b],), (rRL,))
                                stt(K, SC[:, k0:k0 + kn], RL[:, 0:kn], IW[:, i, hh:hh + 1], SC[:, k0:k0 + kn], ALU.mult, ALU.add,
                                    (rRL, rVA, rSCt[w]), (rSCt[w],))
                    mset(K, "dve", SC[0:64, t0 + 64:t0 + 128], NEG, (rSCt[w],))

            def topk_rounds(pr):
                ws = [w for w in range(2) if (2 * pr[w] + 1) * 64 > TOPK]
                rounds = []
                for rd in range(TOPK // 8):
                    def one(ws=ws):
                        for w in ws:
                            _max8(tile_view(SCP, pr, w), M8[w], rSCt[w], rM8[w])
                        for w in ws:
                            _mrep(tile_view(SCP, pr, w), M8[w], rSCt[w], rM8[w])
                    if ws:
                        rounds.append(one)
                return rounds

            def _max8(SC, m8, rsc, rm8):
                K.op("dve", lambda e: e.max(out=m8, in_=SC), (rsc,), (rm8,))

            def _mrep(SC, m8, rsc, rm8):
                K.op("dve", lambda e: e.match_replace(out=SC, in_to_replace=m8, in_values=SC, imm_value=NEG2), (rsc, rm8), (rsc,))

            def masks(pr, par):
                for w in range(2):
                    SC = tile_view(SCP, pr, w)
                    MBt = tile_view(MBP[par], pr, w)
                    if (2 * pr[w] + 1) * 64 > TOPK:
                        ts(K, "dve", MBt, SC, -2.0e30, MASKV, ALU.is_ge, ALU.mult, (rSCt[w],), (rMBt[par][w],))
                    else:
                        ts(K, "dve", MBt, SC, -1.0e29, MASKV, ALU.is_lt, ALU.mult, (rSCt[w],), (rMBt[par][w],))

            def att_unit(pr, par, w, hs, b_, nb):
                i = pr[w]
                t0 = i * 128
                p0 = hs * 64
                kb0 = b_ * 128
                u = cnt["un"]
                cnt["un"] += 1
                pl = 4 + u % 2
                sl = u % 3
                MBt = tile_view(MBP[par], pr, w)

                def stA():
                    mm(K, PSB[pl][:, :], KA2[p0:p0 + 64, kb0:kb0 + 128], QA[p0:p0 + 64, :, t0:t0 + 128], True, False,
                       (rKA, rQA), (rP[pl],))
                    mm(K, PSB[pl][:, :], MBt[:, kb0:kb0 + 128], IREP, False, True, (rMBt[par][w], rC), (rP[pl],))
                    act(K, PT[sl], PSB[pl][:, :], AF.Exp, (rP[pl],), (rPT[sl],), scale=0.125)

                po, pd = (6, 7) if w == 0 else (2, 3)

                def stB():
                    mm(K, PSB[po][p0:p0 + 64, :], VA[:, b_, :], PT[sl], b_ == 0, b_ == nb - 1, (rVA, rPT[sl]), (rP[po],))
                    mm(K, PSB[pd][p0:p0 + 64, :], ONESB[:, 0:64], PT[sl], b_ == 0, b_ == nb - 1, (rC, rPT[sl]), (rP[pd],))
                    if hs == 1 and b_ == nb - 1:
                        recip(K, RC, PSB[pd][:, :], (rP[pd],), (rRC,))
                        tt(K, "dve", YT[:, 0:4, t0:t0 + 128], PSB[po][:, :].rearrange("p (h t) -> p h t", h=4),
                           RC.rearrange("p (h t) -> p h t", h=4), ALU.mult, (rP[po], rRC), (rHT[i],))

                return (stA, stB)

            def att_units(pr, par):
                us = []
                for w in range(2):
                    nb = pr[w] + 1
                    for hs in range(2):
                        for b_ in range(nb):
                            us.append(att_unit(pr, par, w, hs, b_, nb))
                return us

            indexer(pairs[0])
            for rd in topk_rounds(pairs[0]):
                rd()
            masks(pairs[0], 0)
            for pi_, pr in enumerate(pairs):
                nxt = pairs[pi_ + 1] if pi_ + 1 < len(pairs) else None
                if nxt is not None:
                    indexer(nxt)
                us = att_units(pr, pi_ % 2)
                rounds = topk_rounds(nxt) if nxt is not None else []
                n = len(us)
                steps = n + 1
                ri = 0
                if not ILV_DSA:
                    for rd in rounds:
                        rd()
                    rounds = []
                for t in range(steps):
                    if PIPE_DSA:
                        for sidx in range(2):
                            u = t - sidx
                            if 0 <= u < n:
                                us[u][sidx]()
                    elif t < n:
                        us[t][0]()
                        us[t][1]()
                    target = (t + 1) * len(rounds) // steps
                    while ri < target:
                        rounds[ri]()
                        ri += 1
                if nxt is not None:
                    masks(nxt, (pi_ + 1) % 2)

            ph_reset(mark_lat)
            TC_ = ph(S, BF16)
            TS_ = ph(S, BF16)
            rTAB = Res()
            rTAB2 = Res()
            K.dma("sp", TC_, tabs_d[s, 2], (rTABD[s],), (rTAB,))
            K.dma("act", TS_, tabs_d[s, 3], (rTABD[s],), (rTAB2,))
            QBs = [ph(2 * S, BF16).rearrange("p (c t) -> p c t", c=2) for _ in range(2)]
            KBs = [ph(2 * S, BF16).rearrange("p (c t) -> p c t", c=2) for _ in range(2)]
            VBs = [ph(NT * 128, BF16).rearrange("p (i n) -> p i n", i=NT) for _ in range(2)]
            WUQs = [ph(3 * 192, BF16).rearrange("p (k n) -> p k n", k=3) for _ in range(2)]
            WUQSs = [ph(3 * 192, BF16).rearrange("p (k n) -> p k n", k=3) for _ in range(2)]
            WKKs = [ph(2 * 128, BF16).rearrange("p (k n) -> p k n", k=2) for _ in range(2)]
            WKVs = [ph(2 * 128, BF16).rearrange("p (k n) -> p k n", k=2) for _ in range(2)]
            T1 = ph(512, F32)
            T2 = ph(512, F32)
            PT = [ph(512, BF16) for _ in range(2)]
            RC = ph(512, F32)
            rT1, rT2, rRC = Res(), Res(), Res()
            rQBs, rKBs, rVBs = [Res(), Res()], [Res(), Res()], [Res(), Res()]
            rWUQs, rWUQSs, rWKKs, rWKVs = [Res(), Res()], [Res(), Res()], [Res(), Res()], [Res(), Res()]
            rPT = [Res() for _ in range(3)]
            sc_mla = 96.0 ** -0.5
            mcnt = {"zb": 0, "un": 0}

            def mla_w(hp):
                sl_ = hp % 2
                WUQ, WUQS, WKK, WKV = WUQs[sl_], WUQSs[sl_], WKKs[sl_], WKVs[sl_]
                rWUQ, rWUQS, rWKK, rWKV = rWUQs[sl_], rWUQSs[sl_], rWKKs[sl_], rWKVs[sl_]
                K.dma("pool", WUQ, wuq_d[ie][:, hp * 192:(hp + 1) * 192].rearrange("(k p) n -> p k n", p=128), (), (rWUQ,))
                K.dma("pool", WUQS, wuqs_d[ie][:, hp * 192:(hp + 1) * 192].rearrange("(k p) n -> p k n", p=128), (), (rWUQS,))
                K.dma("pool", WKK, wukk_d[ie][:, hp * 128:(hp + 1) * 128].rearrange("(k p) n -> p k n", p=128), (), (rWKK,))
                K.dma("pool", WKV, wukv_d[ie][:, hp * 128:(hp + 1) * 128].rearrange("(k p) n -> p k n", p=128), (), (rWKV,))

            def mla_proj(hp):
                sl_ = hp % 2
                QB, KB, VB = QBs[sl_], KBs[sl_], VBs[sl_]
                WUQ, WUQS, WKK, WKV = WUQs[sl_], WUQSs[sl_], WKKs[sl_], WKVs[sl_]
                rQB, rKB, rVB = rQBs[sl_], rKBs[sl_], rVBs[sl_]
                rWUQ, rWUQS, rWKK, rWKV = rWUQs[sl_], rWUQSs[sl_], rWKKs[sl_], rWKVs[sl_]
                for hh in range(2):
                    for r in range(NR):
                        r0 = r * 512
                        b0 = (mcnt["zb"] % 2) * 2
                        mcnt["zb"] += 1
                        for k in range(3):
                            mm(K, PSB[b0][0:96, :], WUQ[:, k, hh * 96:(hh + 1) * 96], CQN[:, k, r0:r0 + 512], k == 0, k == 2,
                               (rWUQ, rLAT), (rP[b0],))
                        for k in range(3):
                            mm(K, PSB[b0 + 1][0:96, :], WUQS[:, k, hh * 96:(hh + 1) * 96], CQN[:, k, r0:r0 + 512], k == 0, k == 2,
                               (rWUQS, rLAT), (rP[b0 + 1],))
                        cpy(K, "act", QB[0:64, hh, r0:r0 + 512], PSB[b0][0:64, :], (rP[b0],), (rQB,))
                        tt(K, "dve", T1[64:96, :], PSB[b0][64:96, :], TC_[64:96, r0:r0 + 512], ALU.mult, (rP[b0], rTAB), (rT1,))
                        tt(K, "dve", T2[64:96, :], PSB[b0 + 1][64:96, :], TS_[64:96, r0:r0 + 512], ALU.mult, (rP[b0 + 1], rTAB2), (rT2,))
                        tt(K, "pool", QB[64:96, hh, r0:r0 + 512], T1[64:96, :], T2[64:96, :], ALU.add, (rT1, rT2), (rQB,))
                    for r in range(NR):
                        r0 = r * 512
                        b0 = (mcnt["zb"] % 2) * 2
                        mcnt["zb"] += 1
                        for k in range(2):
                            mm(K, PSB[b0][0:64, :], WKK[:, k, hh * 64:(hh + 1) * 64], CKVN[:, k, r0:r0 + 512], k == 0, k == 1,
                               (rWKK, rLAT), (rP[b0],))
                        cpy(K, "act", KB[0:64, hh, r0:r0 + 512], PSB[b0][0:64, :], (rP[b0],), (rKB,))
                    cpy(K, "pool", KB[64:96, hh, :], KRR[64:96, :], (rLAT,), (rKB,))
                for i in range(NT):
                    b0 = (mcnt["zb"] % 2) * 2
                    mcnt["zb"] += 1
                    for k in range(2):
                        mm(K, PSB[b0][:, 0:128], CKVN[:, k, i * 128:(i + 1) * 128], WKV[:, k, :], k == 0, k == 1,
                           (rWKV, rLAT), (rP[b0],))
                    cpy(K, "act", VB[:, i, :], PSB[b0][:, 0:128], (rP[b0],), (rVB,))

            def mla_att(hp):
                sl_ = hp % 2
                units = []
                for r in range(NR):
                    nb = 4 * (r + 1)
                    for hh in range(2):
                        for b_ in range(nb):
                            units.append(_mla_unit(hp, r, hh, b_, nb, mcnt["un"], QBs[sl_], KBs[sl_], VBs[sl_], PT, RC, YT,
                                                   rQBs[sl_], rKBs[sl_], rVBs[sl_], rPT, rRC, sc_mla))
                            mcnt["un"] += 1
                run_pipeline(units, 2, PIPE_MLA)

            mla_w(0)
            mla_w(1)
            mla_proj(0)
            for hp in range(4):
                if hp + 1 < 4:
                    mla_proj(hp + 1)
                if hp + 2 < 4:
                    mla_w(hp + 2)
                mla_att(hp)

            if DEBUG_YT:
                K.barrier()
                ov = out_d[s].rearrange("s d -> (s d)").rearrange("(c p t) -> p c t", c=8, p=128)
                for c in range(8):
                    for t0 in range(0, S, 512):
                        cpy(K, "dve", VT[:, 0, 0:512], YT[:, c, t0:t0 + 512], (rHT[0],), (rVT[0],))
                        K.dma("sp", ov[:, c, t0:t0 + 512], VT[:, 0, 0:512], (rVT[0],), ())
                return
            ph_reset()
            WO = ph(8 * D, BF16).rearrange("p (k n) -> p k n", k=8)
            rWO = Res()
            VT2[0] = ph(D, F32)
            rVT[1] = Res()
            rWOh = [Res(), Res()]
            for h in range(2):
                K.dma("pool", WO[:, :, h * 512:(h + 1) * 512],
                      ewo_d[ie][:, h * 512:(h + 1) * 512].rearrange("(k p) n -> p k n", p=128), (), (rWOh[h],))
            YB = [(4, 5), (2, 3)]
            for i in range(NT + 2):
                if i < NT:
                    yb = YB[i % 2]
                    for h in range(2):
                        for k in range(8):
                            mm(K, PSB[yb[h]][:, :], YT[:, k, i * 128:(i + 1) * 128], WO[:, k, h * 512:(h + 1) * 512],
                               k == 0, k == 7, (rHT[i], rWOh[h]), (rP[yb[h]],))
                if 1 <= i <= NT:
                    epilogue(i - 1, YB[(i - 1) % 2], s, (l, 1), False)
                if i >= 2:
                    epilogue_b(i - 2, s, (l, 1), False)

        for s in range(NSEQ):
            for i in range(NT):
                K.dma("sp" if i % 2 == 0 else "act", X[:, i, :], x_d[s, i * 128:(i + 1) * 128, :], (), (rX[i],))
            first = (0, 0) if kinds[0] != "none" else (0, 1)
            for i in range(NT):
                transpose_mod(i, first[0], first[1], s)
            if "even" in kinds:
                ph_reset()
                NTC = min(1024, S)
                tTC, tTS = ph(S, BF16), ph(S, BF16)
                tTa, tTb, tTc = ph(NTC, F32), ph(NTC, F32), ph(NTC, F32)
                tTi = ph(NTC, F32).bitcast(I32)
                rTT_, rTmp_ = Res(), Res()
                for half, (FQ, SG) in enumerate(((FREQA, SGNA), (FREQB, SGNB))):
                    make_tables(s, tTC, tTS, FQ, SG, rTT_, tTa, tTb, tTc, tTi, rTmp_)
                    K.dma("sp", tabs_d[s, 2 * half], tTC, (rTT_,), (rTABD[s],))
                    K.dma("sp", tabs_d[s, 2 * half + 1], tTS, (rTT_,), (rTABD[s],))
            for l in range(L):
                if kinds[l] == "even":
                    even_mixer(l, s)
                elif kinds[l] == "odd":
                    odd_mixer(l, s)
                lastl = l == L - 1
                if DEBUG_YT:
                    continue
                nxt = None if lastl else ((l + 1, 0) if kinds[l + 1] != "none" else (l + 1, 1))
                ffn(l, s, nxt, lastl)
                K.flush()
        K.flush(final=True)
    return nc, K


NX_COLS = 24 * 128
DEBUG_YT = False
DBG_MODE = 0
PIPE_ODD = True
PIPE_MLA = True
PIPE_DSA = True
ILV_DSA = False


def _swap_half(w, lo, n):
    h = n // 2
    return np.concatenate([w[:, lo + h:lo + n], w[:, lo:lo + h]], axis=1)


def host_consts():
    p = np.arange(128)
    cf = np.zeros((128, 3 * 128 + 8), np.float32)
    cf[:, 0:128] = np.eye(128, dtype=np.float32)
    cf[:, 128:256] = 1.0
    cf[:, 256:384] = (p[:, None] // 64 == p[None, :] // 64).astype(np.float32) / 64.0
    inv64 = (np.float32(10000.0) ** (-np.arange(0, 64, 2, dtype=np.float32) / np.float32(64))).astype(np.float32)
    inv32 = (np.float32(10000.0) ** (-np.arange(0, 32, 2, dtype=np.float32) / np.float32(32))).astype(np.float32)
    cf[:, 384] = inv64[(p % 64) % 32]
    cf[:, 385] = np.where((p % 64) < 32, -1.0, 1.0)
    cf[:, 386] = inv32[(p % 32) % 16]
    cf[:, 387] = np.where((p % 32) < 16, -1.0, 1.0)
    cb = np.zeros((128, 4 * 128 + 512), np.float32)
    cb[:, 0:128] = 1.0
    cb[:, 128:256] = -1.0
    cb[:, 256:384] = -(p[:, None] >= p[None, :]).astype(np.float32)
    cb[:, 384:512] = (p[:, None] < p[None, :]).astype(np.float32)
    cb[:, 512:1024] = np.tile(np.eye(128, dtype=np.float32), (1, 4))
    return cf, cb


def host_prep(inp, kinds):
    L = len(kinds)
    f = lambda a: np.ascontiguousarray(np.asarray(a, dtype=np.float32))
    sh = {}
    sh["mod_w"] = f(inp["mod_w"])[:L]
    sh["mod_bT"] = f(np.asarray(inp["mod_b"])[:L].reshape(L, 48, 128).transpose(0, 2, 1))
    sh["lnp"] = f(np.stack([np.asarray(inp["ln_mix_g"])[:L], np.asarray(inp["ln_mix_b"])[:L],
                            np.asarray(inp["ln_ffn_g"])[:L], np.asarray(inp["ln_ffn_b"])[:L]], axis=1))
    ne = max(1, sum(1 for k in kinds if k == "even"))
    no = max(1, sum(1 for k in kinds if k == "odd"))
    wins, smalls, wuq, wuqs, wkk, wkv = [], [], [], [], [], []
    for i in range(ne):
        w = np.asarray(inp["ev_w_in"][i], dtype=np.float32)
        o_qa, o_ka, o_va, o_iq, o_ik, o_iw, o_cq, o_ckv, o_kr = 0, 512, 576, 640, 896, 960, 964, 1348, 1604
        blocks = []
        for j in range(4):
            blocks.append(w[:, o_qa + j * 128:o_qa + (j + 1) * 128])
        for j in range(4):
            blocks.append(np.concatenate([_swap_half(w, o_qa + (2 * j) * 64, 64), _swap_half(w, o_qa + (2 * j + 1) * 64, 64)], axis=1))
        ka = w[:, o_ka:o_ka + 64]
        kas = _swap_half(w, o_ka, 64)
        blocks += [np.concatenate([ka, ka], 1), np.concatenate([kas, kas], 1)]
        for j in range(2):
            blocks.append(w[:, o_iq + j * 128:o_iq + (j + 1) * 128])
        for j in range(2):
            blocks.append(np.concatenate([_swap_half(w, o_iq + (2 * j) * 64, 64), _swap_half(w, o_iq + (2 * j + 1) * 64, 64)], axis=1))
        ik = w[:, o_ik:o_ik + 64]
        iks = _swap_half(w, o_ik, 64)
        blocks += [np.concatenate([ik, ik], 1), np.concatenate([iks, iks], 1)]
        for j in range(3):
            blocks.append(w[:, o_cq + j * 128:o_cq + (j + 1) * 128])
        for j in range(2):
            blocks.append(w[:, o_ckv + j * 128:o_ckv + (j + 1) * 128])
        kr = w[:, o_kr:o_kr + 32]
        krs = _swap_half(w, o_kr, 32)
        blocks += [np.concatenate([ka, kr, kr], 1), np.concatenate([ka, krs, krs], 1)]
        blocks.append(np.concatenate([w[:, o_va:o_va + 64], w[:, o_iw:o_iw + 4], w[:, 0:60]], 1))
        wx = np.concatenate(blocks, axis=1)
        assert wx.shape == (1024, NX_COLS), wx.shape
        wins.append(wx)
        sm = np.zeros((128, 16), np.float32)
        g = np.asarray(inp["ev_idx_k_g"][i], np.float32)
        b = np.asarray(inp["ev_idx_k_b"][i], np.float32)
        p = np.arange(128)
        sm[:, 0] = g[p % 64]
        sm[:, 1] = b[p % 64]
        sm[:, 2] = g[(p % 64 + 32) % 64]
        sm[:, 3] = b[(p % 64 + 32) % 64]
        sm[:, 4:7] = np.asarray(inp["ev_q_norm_g"][i], np.float32).reshape(3, 128).T
        sm[:, 7:9] = np.asarray(inp["ev_kv_norm_g"][i], np.float32).reshape(2, 128).T
        smalls.append(sm)
        uq = np.asarray(inp["ev_w_uq"][i], np.float32)
        uqs = uq.copy()
        for h in range(8):
            uqs[:, h * 96 + 64:h * 96 + 96] = _swap_half(uq, h * 96 + 64, 32)
        wuq.append(uq)
        wuqs.append(uqs)
        ukv = np.asarray(inp["ev_w_ukv"][i], np.float32).reshape(256, 8, 128)
        wkk.append(ukv[:, :, 0:64].reshape(256, 512))
        wkv.append(ukv[:, :, 64:128].reshape(256, 512))
    sh["ev_win"] = f(np.stack(wins))
    sh["ev_small"] = f(np.stack(smalls))
    sh["ev_wuq"] = f(np.stack(wuq))
    sh["ev_wuq_sw"] = f(np.stack(wuqs))
    sh["ev_wukv_k"] = f(np.stack(wkk))
    sh["ev_wukv_v"] = f(np.stack(wkv))
    sh["ev_wout"] = f(np.asarray(inp["ev_w_out"])[:ne])
    sh["od_wqkv"] = f(np.asarray(inp["od_w_qkv"])[:no])
    sh["od_wout"] = f(np.asarray(inp["od_w_out"])[:no])
    sh["ffn_wup"] = f(np.asarray(inp["ffn_w_up"])[:L])
    sh["ffn_cw"] = f(np.asarray(inp["ffn_conv_w"])[:L].transpose(0, 2, 1).reshape(L, NFC, 128, 3).transpose(0, 2, 1, 3))
    sh["ffn_cb"] = f(np.asarray(inp["ffn_conv_b"])[:L].reshape(L, NFC, 128).transpose(0, 2, 1))
    sh["ffn_wdown"] = f(np.asarray(inp["ffn_w_down"])[:L])
    cf, cb = host_consts()
    sh["cf32"] = cf
    sh["cbf"] = cb
    return sh


def core_inputs(inp, shared, seqs):
    m = dict(shared)
    m["x"] = np.ascontiguousarray(np.asarray(inp["x"], np.float32)[seqs])
    cT = np.asarray(inp["c"], np.float32)[seqs].T
    m["cT"] = np.ascontiguousarray(cT.reshape(8, 128, len(seqs)).transpose(1, 0, 2))
    m["pos"] = np.ascontiguousarray(np.asarray(inp["positions"], np.int32)[seqs])
    return m


_CACHE = {}


def run(inp, S, NSEQ, kinds, TOPK, ncores):
    key = (S, NSEQ, tuple(kinds), TOPK)
    if key not in _CACHE:
        _CACHE[key] = build(S, NSEQ, kinds, TOPK)[0]
    nc = _CACHE[key]
    shared = host_prep(inp, kinds)
    in_maps = [core_inputs(inp, shared, list(range(c * NSEQ, (c + 1) * NSEQ))) for c in range(ncores)]
    res = run_bass_kernel_spmd(nc, in_maps, core_ids=list(range(ncores)))
    return np.concatenate([np.asarray(r["out"]) for r in res.results], axis=0)


def kernel(**inputs):
    out = run(inputs, 2048, 2, ["even", "odd", "even", "odd"], 256, 8)
    return out.astype(np.float32)
```
